# Optimizing a Trainium2 kernel written in Bass

```python
import math
import jax, jax.numpy as jnp
from jax import lax
import numpy as np

D_MODEL = 1024
BATCH = 16
SEQ = 2048
DEPTH = 2

HEAD_DIM = 64
DIFF_HEADS = 4
DIFF_QK = 2 * HEAD_DIM
DIFF_V = 2 * HEAD_DIM
Q_BLOCK = 128
GDN_HEADS = 4
GDN_DK = 128
GDN_DV = 128
GDN_QKV = GDN_HEADS * (2 * GDN_DK + GDN_DV)
GDN_CONV = 5
GDN_CHUNK = 64
SWA_HEADS = 8
SWA_KV_HEADS = 2
SWA_WINDOW = 128
SWA_BLOCK = 128
SWA_SIDE_BLOCKS = SWA_WINDOW // SWA_BLOCK
N_BRANCHES = 3
BRANCH_WIDTH = 512
REL_BUCKETS = 32
REL_MAX_DIST = 128
REL_HEADS = DIFF_HEADS + SWA_HEADS
D_FF = 2816
FFN_CONV = 3
DEEPNORM_ALPHA = (2 * DEPTH) ** 0.25
DEEPNORM_BETA = (8 * DEPTH) ** -0.25
LN_EPS = 1e-5
RMS_EPS = 1e-6

COLS = (
    DIFF_HEADS * DIFF_QK,
    DIFF_HEADS * DIFF_QK,
    DIFF_HEADS * DIFF_V,
    GDN_QKV,
    GDN_HEADS * GDN_DV,
    2 * GDN_HEADS,
    2 * GDN_HEADS,
    SWA_HEADS * HEAD_DIM,
    SWA_KV_HEADS * HEAD_DIM,
    SWA_KV_HEADS * HEAD_DIM,
    N_BRANCHES * D_MODEL,
)
D_IN = sum(COLS)
SPLITS = tuple(sum(COLS[:i + 1]) for i in range(len(COLS) - 1))

kernel_name = 'hybrid_gated_diff_gdn_swa_encoder'


def layer_norm(x, g, b):
    xf = x.astype(jnp.float32)
    mu = jnp.mean(xf, axis=-1, keepdims=True)
    var = jnp.mean(jnp.square(xf - mu), axis=-1, keepdims=True)
    return ((xf - mu) * lax.rsqrt(var + LN_EPS) * g + b).astype(x.dtype)


def rms_norm(x, g):
    xf = x.astype(jnp.float32)
    return (xf * lax.rsqrt(jnp.mean(xf * xf, axis=-1, keepdims=True) + RMS_EPS) * g).astype(x.dtype)


def l2_normalize(x):
    return x * lax.rsqrt(jnp.sum(x * x, axis=-1, keepdims=True) + RMS_EPS)


def dwconv_centred(x, w):
    K, C = w.shape
    return lax.conv_general_dilated(
        x, w[:, None, :].astype(x.dtype), window_strides=(1,),
        padding=((K // 2, K // 2),), dimension_numbers=('NWC', 'WIO', 'NWC'),
        feature_group_count=C)


def t5_bucket(rel):
    nb = REL_BUCKETS // 2
    ret = jnp.where(rel > 0, nb, 0)
    n = jnp.abs(rel)
    max_exact = nb // 2
    large = max_exact + (jnp.log(jnp.maximum(n, 1).astype(jnp.float32) / max_exact)
                         / math.log(REL_MAX_DIST / max_exact) * (nb - max_exact)).astype(jnp.int32)
    large = jnp.minimum(large, nb - 1)
    return ret + jnp.where(n < max_exact, n, large)


def diff_attention(q, k, v, lam, bias_table):
    B, T, H, _, dh = q.shape
    nb = T // Q_BLOCK
    scale = dh ** -0.5
    qb = q.reshape(B, nb, Q_BLOCK, H, 2, dh).transpose(1, 0, 3, 4, 2, 5)
    kt = k.transpose(0, 2, 3, 1, 4)
    vt = v.transpose(0, 2, 1, 3)
    key_pos = jnp.arange(T)

    def block(args):
        qi, start = args
        s = jnp.einsum('bhcqd,bhckd->bhcqk', qi, kt).astype(jnp.float32) * scale
        rel = key_pos[None, :] - (start + jnp.arange(Q_BLOCK))[:, None]
        bias = bias_table[t5_bucket(rel)].astype(jnp.float32).transpose(2, 0, 1)
        p = jax.nn.softmax(s + bias[None, :, None], axis=-1)
        a = p[:, :, 0] - lam * p[:, :, 1]
        return jnp.einsum('bhqk,bhkd->bhqd', a.astype(vt.dtype), vt)

    starts = jnp.arange(nb) * Q_BLOCK
    o = lax.map(block, (qb, starts))
    return o.transpose(1, 0, 3, 2, 4).reshape(B, T, H, -1)


def diff_mixer(qa, ka, va, lam_vecs, subln, bias_table, layer_idx):
    B, T, _ = qa.shape
    q = qa.reshape(B, T, DIFF_HEADS, 2, HEAD_DIM)
    k = ka.reshape(B, T, DIFF_HEADS, 2, HEAD_DIM)
    v = va.reshape(B, T, DIFF_HEADS, DIFF_V)
    lam_init = 0.8 - 0.6 * math.exp(-0.3 * layer_idx)
    lv = lam_vecs.astype(jnp.float32)
    lam = jnp.exp(jnp.dot(lv[0], lv[1])) - jnp.exp(jnp.dot(lv[2], lv[3])) + lam_init
    o = diff_attention(q, k, v, lam, bias_table)
    o = rms_norm(o, subln) * (1.0 - lam_init)
    return o.reshape(B, T, DIFF_HEADS * DIFF_V)


def gated_delta_chunked(q, k, v, g, beta):
    B, T, H, dk = q.shape
    dv = v.shape[-1]
    C = GDN_CHUNK
    N = T // C

    def chunks(t):
        return jnp.moveaxis(t.reshape((B, N, C, H) + t.shape[3:]), 3, 1)

    q = chunks(q * dk ** -0.5)
    k = chunks(k)
    v = chunks(v)
    gc = jnp.cumsum(chunks(g), axis=-1)
    beta = chunks(beta)
    kb = k * beta[..., None]
    idx = jnp.arange(C)
    incl = idx[:, None] >= idx[None, :]
    strict = idx[:, None] > idx[None, :]
    decay = jnp.exp(jnp.where(incl, gc[..., :, None] - gc[..., None, :], -jnp.inf))
    kk = jnp.einsum('bhncd,bhnsd->bhncs', kb, k) * decay
    a_mat = jnp.where(strict, kk, 0.0) + jnp.eye(C, dtype=kk.dtype)
    rhs = jnp.concatenate([v * beta[..., None], kb * jnp.exp(gc)[..., None]], axis=-1)
    sol = lax.linalg.triangular_solve(a_mat, rhs, left_side=True, lower=True, unit_diagonal=True)
    u, w = sol[..., :dv], sol[..., dv:]
    qk = jnp.where(incl, jnp.einsum('bhncd,bhnsd->bhncs', q, k) * decay, 0.0)

    def step(S, inp):
        q_i, k_i, u_i, w_i, g_i, qk_i = inp
        v_new = u_i - jnp.einsum('bhcd,bhde->bhce', w_i, S)
        o_i = (jnp.einsum('bhcd,bhde->bhce', q_i * jnp.exp(g_i)[..., None], S)
               + jnp.einsum('bhcs,bhse->bhce', qk_i, v_new))
        g_last = g_i[..., -1:]
        S = (S * jnp.exp(g_last)[..., None]
             + jnp.einsum('bhcd,bhce->bhde', k_i * jnp.exp(g_last - g_i)[..., None], v_new))
        return S, o_i

    xs = tuple(jnp.moveaxis(t, 2, 0) for t in (q, k, u, w, gc, qk))
    S0 = jnp.zeros((B, H, dk, dv), q.dtype)
    _, o = lax.scan(step, S0, xs)
    return jnp.moveaxis(o, 0, 2).transpose(0, 2, 3, 1, 4).reshape(B, T, H, dv)


def gdn_mixer(qkv, gate, a_in, b_in, conv_w, a_log, dt_bias, norm_w):
    B, T, _ = qkv.shape
    h = jax.nn.silu(dwconv_centred(qkv, conv_w)).astype(jnp.float32)
    q, k, v = jnp.split(h, [GDN_HEADS * GDN_DK, 2 * GDN_HEADS * GDN_DK], axis=-1)
    q = l2_normalize(q.reshape(B, T, GDN_HEADS, GDN_DK))
    k = l2_normalize(k.reshape(B, T, GDN_HEADS, GDN_DK))
    v = v.reshape(B, T, GDN_HEADS, GDN_DV)
    a = a_in.astype(jnp.float32).reshape(B, T, 2, GDN_HEADS)
    b = b_in.astype(jnp.float32).reshape(B, T, 2, GDN_HEADS)
    g = -jnp.exp(a_log.astype(jnp.float32)) * jax.nn.softplus(a + dt_bias.astype(jnp.float32))
    beta = jax.nn.sigmoid(b)
    o_fwd = gated_delta_chunked(q, k, v, g[:, :, 0], beta[:, :, 0])
    flip = lambda t: jnp.flip(t, axis=1)
    o_bwd = flip(gated_delta_chunked(flip(q), flip(k), flip(v), flip(g[:, :, 1]), flip(beta[:, :, 1])))
    o = rms_norm(o_fwd + o_bwd, norm_w) * jax.nn.silu(
        gate.astype(jnp.float32).reshape(B, T, GDN_HEADS, GDN_DV))
    return o.reshape(B, T, GDN_HEADS * GDN_DV).astype(qkv.dtype)


def window_gqa(q, k, v, sink, bias_table):
    B, T, Hq, dh = q.shape
    G = k.shape[2]
    R = Hq // G
    QB = SWA_BLOCK
    W = SWA_WINDOW
    side = SWA_SIDE_BLOCKS
    nb = T // QB
    KB = QB * (2 * side + 1)
    pad = ((0, 0), (W, W), (0, 0), (0, 0))
    kp = jnp.pad(k, pad).reshape(B, nb + 2 * side, QB, G, dh)
    vp = jnp.pad(v, pad).reshape(B, nb + 2 * side, QB, G, dh)
    kw = jnp.concatenate([kp[:, i:i + nb] for i in range(2 * side + 1)], axis=2)
    vw = jnp.concatenate([vp[:, i:i + nb] for i in range(2 * side + 1)], axis=2)
    qb = q.reshape(B, nb, QB, G, R, dh)
    s = jnp.einsum('bnqgrd,bnkgd->bngrqk', qb, kw).astype(jnp.float32) * dh ** -0.5
    rel = jnp.arange(KB)[None, :] - W - jnp.arange(QB)[:, None]
    bias = bias_table[t5_bucket(rel)].astype(jnp.float32).transpose(2, 0, 1).reshape(G, R, QB, KB)
    kabs = (jnp.arange(nb) * QB)[:, None] - W + jnp.arange(KB)[None, :]
    valid = (jnp.abs(rel) <= W)[None] & ((kabs >= 0) & (kabs < T))[:, None, :]
    s = jnp.where(valid[None, :, None, None], s + bias[None, None], -jnp.inf)
    sink_f = sink.astype(jnp.float32).reshape(G, R)[None, None, :, :, None, None]
    m = jnp.maximum(jnp.max(s, axis=-1, keepdims=True), sink_f)
    e = jnp.exp(s - m)
    p = e / (jnp.sum(e, axis=-1, keepdims=True) + jnp.exp(sink_f - m))
    o = jnp.einsum('bngrqk,bnkgd->bnqgrd', p.astype(vw.dtype), vw)
    return o.reshape(B, T, Hq * dh)


def hybrid_mixer(x, rel_bias, w_in, diff_lambda, diff_subln, gdn_conv, gdn_a_log, gdn_dt_bias,
                 gdn_norm, swa_sink, w_branch, w_out, layer_idx):
    B, T, D = x.shape
    z = x @ w_in
    aq, ak, av, bqkv, bgate, ba, bb, cq, ck, cv, gz = jnp.split(z, SPLITS, axis=-1)
    y_a = diff_mixer(aq, ak, av, diff_lambda, diff_subln, rel_bias[:, :DIFF_HEADS], layer_idx)
    y_b = gdn_mixer(bqkv, bgate, ba, bb, gdn_conv, gdn_a_log, gdn_dt_bias, gdn_norm)
    y_c = window_gqa(cq.reshape(B, T, SWA_HEADS, HEAD_DIM),
                     ck.reshape(B, T, SWA_KV_HEADS, HEAD_DIM),
                     cv.reshape(B, T, SWA_KV_HEADS, HEAD_DIM),
                     swa_sink, rel_bias[:, DIFF_HEADS:])
    ys = jnp.stack([y_a, y_b, y_c], axis=2)
    proj = jnp.einsum('btnc,ncd->btnd', ys, w_branch)
    gates = jax.nn.sigmoid(gz.reshape(B, T, N_BRANCHES, D))
    merged = jnp.sum(gates * proj, axis=2)
    return merged @ w_out


def conv_glu_ffn(x, w_up, conv_w, conv_b, w_down):
    gate, up = jnp.split(x @ w_up, 2, axis=-1)
    gate = dwconv_centred(gate, conv_w) + conv_b
    return (jax.nn.silu(gate) * up) @ w_down


def setup_inputs(seed: int = 0) -> dict:
    key = jax.random.key(seed)
    ks = jax.random.split(key, 24)
    L = DEPTH

    def nrm(k, shape, scale):
        return jax.random.normal(k, shape, jnp.float32) * scale

    x = nrm(ks[0], (BATCH, SEQ, D_MODEL), 1.0)
    rel_bias = nrm(ks[1], (REL_BUCKETS, REL_HEADS), 0.5)
    w_in = nrm(ks[2], (L, D_MODEL, D_IN), D_MODEL ** -0.5)
    diff_lambda = nrm(ks[3], (L, 4, HEAD_DIM), 0.1)
    diff_subln = 1.0 + nrm(ks[4], (L, DIFF_V), 0.02)
    gdn_conv = nrm(ks[5], (L, GDN_CONV, GDN_QKV), GDN_CONV ** -0.5)
    gdn_a_log = jnp.log(jax.random.uniform(ks[6], (L, 2, GDN_HEADS), jnp.float32, 1.0, 16.0))
    dt = jnp.exp(jax.random.uniform(ks[7], (L, 2, GDN_HEADS), jnp.float32,
                                    math.log(1e-3), math.log(1e-1)))
    gdn_dt_bias = dt + jnp.log(-jnp.expm1(-dt))
    gdn_norm = 1.0 + nrm(ks[8], (L, GDN_DV), 0.02)
    swa_sink = nrm(ks[9], (L, SWA_HEADS), 0.5)
    w_branch = nrm(ks[10], (L, N_BRANCHES, BRANCH_WIDTH, D_MODEL), BRANCH_WIDTH ** -0.5)
    w_out = nrm(ks[11], (L, D_MODEL, D_MODEL), D_MODEL ** -0.5 * DEEPNORM_BETA)
    ln1_g = 1.0 + nrm(ks[12], (L, D_MODEL), 0.02)
    ln1_b = nrm(ks[13], (L, D_MODEL), 0.02)
    ffn_up = nrm(ks[14], (L, D_MODEL, 2 * D_FF), D_MODEL ** -0.5)
    ffn_conv = nrm(ks[15], (L, FFN_CONV, D_FF), FFN_CONV ** -0.5)
    ffn_conv_b = nrm(ks[16], (L, D_FF), 0.02)
    ffn_down = nrm(ks[17], (L, D_FF, D_MODEL), D_FF ** -0.5 * DEEPNORM_BETA)
    ln2_g = 1.0 + nrm(ks[18], (L, D_MODEL), 0.02)
    ln2_b = nrm(ks[19], (L, D_MODEL), 0.02)
    return {'x': x, 'rel_bias': rel_bias, 'w_in': w_in, 'diff_lambda': diff_lambda,
            'diff_subln': diff_subln, 'gdn_conv': gdn_conv, 'gdn_a_log': gdn_a_log,
            'gdn_dt_bias': gdn_dt_bias, 'gdn_norm': gdn_norm, 'swa_sink': swa_sink,
            'w_branch': w_branch, 'w_out': w_out, 'ln1_g': ln1_g, 'ln1_b': ln1_b,
            'ffn_up': ffn_up, 'ffn_conv': ffn_conv, 'ffn_conv_b': ffn_conv_b,
            'ffn_down': ffn_down, 'ln2_g': ln2_g, 'ln2_b': ln2_b}


def reference(x, rel_bias, w_in, diff_lambda, diff_subln, gdn_conv, gdn_a_log, gdn_dt_bias,
              gdn_norm, swa_sink, w_branch, w_out, ln1_g, ln1_b, ffn_up, ffn_conv, ffn_conv_b,
              ffn_down, ln2_g, ln2_b):
    for l in range(DEPTH):
        h = hybrid_mixer(x, rel_bias, w_in[l], diff_lambda[l], diff_subln[l], gdn_conv[l],
                         gdn_a_log[l], gdn_dt_bias[l], gdn_norm[l], swa_sink[l], w_branch[l],
                         w_out[l], l)
        x = layer_norm(DEEPNORM_ALPHA * x + h, ln1_g[l], ln1_b[l])
        f = conv_glu_ffn(x, ffn_up[l], ffn_conv[l], ffn_conv_b[l], ffn_down[l])
        x = layer_norm(DEEPNORM_ALPHA * x + f, ln2_g[l], ln2_b[l])
    return x
```

```python
import math
import contextlib
import numpy as np
import concourse.bass as bass
import concourse.mybir as mybir
from concourse.bass_utils import run_bass_kernel_spmd

F32 = mybir.dt.float32
BF16 = mybir.dt.bfloat16
AF = mybir.ActivationFunctionType
ALU = mybir.AluOpType
AX = mybir.AxisListType

ENGS = ("pe", "act", "dve", "pool", "sp")
SEM_LIMIT = 30000
import re
PSKEY = re.compile(r"ps[A-Za-z]|_po\d|_pb\d")
import os
MAXOPS = int(os.environ.get('KB_MAXOPS', '1000000000'))


class KB:
    def __init__(self, n_dma_slots=24, n_epochs=6):
        self.nc = bass.Bass("TRN2", target_bir_lowering=False)
        self.stack = contextlib.ExitStack()
        self.stream = {e: [] for e in ENGS}
        self.n_epochs = n_epochs
        self.csem = {}
        for e in ("pe", "act", "dve", "pool"):
            self.csem[e] = [self.stack.enter_context(self.nc.semaphore(f"c_{e}_{i}")) for i in range(n_epochs)]
        self.epoch = {e: 0 for e in self.csem}
        self.cnt = {e: 0 for e in self.csem}
        self.dsem = [self.stack.enter_context(self.nc.semaphore(f"d_{i}")) for i in range(n_dma_slots)]
        self.dval = [0] * n_dma_slots
        self.dnext = 0
        self.waited = {e: {} for e in ENGS}
        self.lastw = {}
        self.reads = {}
        self.final_tokens = []
        self.ninst = {e: 0 for e in ENGS}

    def sb(self, name, shape, dtype):
        self.uid = getattr(self, "uid", 0) + 1
        return self.stack.enter_context(self.nc.sbuf_tensor(f"{name}_u{self.uid}", list(shape), dtype))

    def ps(self, name, shape, dtype=F32):
        self.uid = getattr(self, "uid", 0) + 1
        return self.stack.enter_context(self.nc.psum_tensor(f"{name}_u{self.uid}", list(shape), dtype))

    def dram(self, name, shape, dtype, kind="Internal"):
        return self.nc.dram_tensor(name, list(shape), dtype, kind=kind)

    def _wait(self, eng, tok):
        sem, val, sid = tok
        w = self.waited[eng]
        if w.get(sid, 0) >= val:
            return
        w[sid] = val
        self.stream[eng].append(("wait", sem, val))

    def _deps(self, eng, R, W):
        toks = []
        for r in R:
            t = self.lastw.get(r)
            if t is not None:
                toks.append(t)
            if isinstance(r, str) and PSKEY.search(r):
                for t in self.reads.get(r, {}).values():
                    if t[3] != eng:
                        toks.append(t)
        for w in W:
            t = self.lastw.get(w)
            if t is not None:
                toks.append(t)
            for t in self.reads.get(w, {}).values():
                toks.append(t)
        for t in toks:
            src = t[3]
            if src == eng and eng == "pe":
                continue
            self._wait(eng, t[:3])

    def _record(self, tok, R, W):
        for r in R:
            self.reads.setdefault(r, {})[tok[3] + str(tok[2])] = tok
        for w in W:
            self.lastw[w] = tok
            self.reads[w] = {}

    def op(self, eng, fn, R=(), W=()):
        self.opcount = getattr(self, "opcount", 0) + 1
        if os.environ.get("KB_LOG"):
            print("OP", self.opcount, eng, list(R), list(W))
        if self.opcount > MAXOPS:
            return None
        self._deps(eng, R, W)
        if self.cnt[eng] >= SEM_LIMIT:
            self.epoch[eng] += 1
            self.cnt[eng] = 0
        self.cnt[eng] += 1
        ep = self.epoch[eng]
        sem = self.csem[eng][ep]
        tok = (sem, self.cnt[eng], f"c_{eng}_{ep}", eng)
        self.stream[eng].append(("op", fn, sem, 1))
        self._record(tok, R, W)
        self.ninst[eng] += 1
        return tok

    def dma(self, q, out, in_, R=(), W=(), final=False, **kw):
        self.opcount = getattr(self, "opcount", 0) + 1
        if os.environ.get("KB_LOG"):
            print("OP", self.opcount, "dma-" + q, list(R), list(W))
        if self.opcount > MAXOPS and not final:
            return None
        self._deps(q, R, W)
        nsl = len(self.dsem)
        half = nsl // 2
        if q == "pool":
            self.dnext_sw = (getattr(self, "dnext_sw", -1) + 1) % (nsl - half)
            s = half + self.dnext_sw
        else:
            self.dnext = (self.dnext + 1) % half
            s = self.dnext
        sem = self.dsem[s]
        if self.dval[s] > 0:
            self._wait(q, (sem, self.dval[s], f"d_{s}"))
        self.dval[s] += 16
        tok = (sem, self.dval[s], f"d_{s}", "dma")
        self.stream[q].append(("dma", out, in_, kw, sem))
        self._record(tok, R, W)
        if final:
            self.final_tokens.append(tok)
        self.ninst[q] += 1
        return tok

    def barrier(self, engs=ENGS):
        for q in engs:
            for e in self.csem:
                if e == q:
                    continue
                ep = self.epoch[e]
                if self.cnt[e] > 0:
                    self._wait(q, (self.csem[e][ep], self.cnt[e], f"c_{e}_{ep}"))
                for pe_ in range(ep):
                    self._wait(q, (self.csem[e][pe_], SEM_LIMIT, f"c_{e}_{pe_}"))
            for s in range(len(self.dsem)):
                if self.dval[s] > 0:
                    self._wait(q, (self.dsem[s], self.dval[s], f"d_{s}"))

    def phase(self):
        kb = self

        class _P:
            def __enter__(s_):
                s_.outer = kb.stack
                kb.stack = contextlib.ExitStack()
                return s_

            def __exit__(s_, et, ev, tb):
                if et is None:
                    kb.barrier()
                    kb.emit()
                kb.stack.close()
                kb.stack = s_.outer
                return False

        return _P()

    def finish(self):
        self.stack.close()
        self.nc._kb_ninst = dict(self.ninst)
        self.nc._kb_opcount = getattr(self, 'opcount', 0)
        return self.nc

    def emit(self):
        nc = self.nc
        streams = self.stream
        self.stream = {e: [] for e in ENGS}
        with nc.Block() as block:
            def replay(engobj, items):
                for it in items:
                    if it[0] == "wait":
                        engobj.wait_ge(it[1], it[2])
                    elif it[0] == "op":
                        ins = it[1](engobj)
                        ins.then_inc(it[2], it[3])
                    else:
                        _, out, in_, kw, sem = it
                        engobj.dma_start(out=out, in_=in_, **kw).then_inc(sem, 16)

            @block.tensor
            def _(e):
                replay(e, streams["pe"])

            @block.scalar
            def _(e):
                replay(e, streams["act"])

            @block.vector
            def _(e):
                replay(e, streams["dve"])

            @block.gpsimd
            def _(e):
                replay(e, streams["pool"])

            @block.sync
            def _(e):
                replay(e, streams["sp"])


D = 1024
KC = 8
DIN = 7440
DFF = 2816
NFC = 22
DEPTH = 2
ALPHA = (2 * DEPTH) ** 0.25
LN_EPS = 1e-5
RMS_EPS = 1e-6
AQ, AK, AV, BQKV, BG, BA, BB, CQ, CK, CV, GZ = 0, 512, 1024, 1536, 3072, 3584, 3592, 3600, 4112, 4240, 4368
SA_W = 1408
NEG = -30000.0
C_ID, C_TRIF, C_TRIB, C_ONES, C_MAF, C_MAB, C_MDF, C_MDB = [i * 128 for i in range(8)]


def make_consts():
    c = np.zeros((128, 8 * 128), np.float32)
    p = np.arange(128)[:, None]
    f = np.arange(128)[None, :]
    c[:, C_ID:C_ID + 128] = (p == f)
    c[:, C_TRIF:C_TRIF + 128] = (p <= f)
    c[:, C_TRIB:C_TRIB + 128] = (p >= f)
    c[:, C_ONES:C_ONES + 128] = 1.0
    c[:, C_MAF:C_MAF + 128] = np.where(f >= p, 0.0, NEG)
    c[:, C_MAB:C_MAB + 128] = np.where(f <= p, 0.0, NEG)
    c[:, C_MDF:C_MDF + 128] = np.where(f < p, 0.0, -NEG)
    c[:, C_MDB:C_MDB + 128] = np.where(f > p, 0.0, -NEG)
    return c


class Ctx:
    pass


def build(T, NS, L, dbg=False, phases="0abcmf"):
    NT = T // 128
    QC = min(512, T)
    NQC = T // QC
    TB = min(512, T)
    NTB = T // TB
    TP = T + 4
    k = KB()
    nc = k.nc
    C = Ctx()
    C.T, C.NT, C.QC, C.NQC, C.TB, C.NTB, C.TP = T, NT, QC, NQC, TB, NTB, TP

    def din(name, shape, dt=F32):
        return nc.dram_tensor(name, list(shape), dt, kind="ExternalInput").ap()

    C.x = din("x", [NS, T, D])
    C.w_in = din("w_in", [L, D, DIN])
    C.w_ck = din("w_ck", [L, D, 256])
    C.w_br = din("w_branch", [L, 3, 512, D])
    C.w_out = din("w_out", [L, D, D])
    C.w_up = din("ffn_up", [L, D, 2 * DFF])
    C.w_dn = din("ffn_down", [L, DFF, D])
    C.lnp = din("lnp", [L, 4, D])
    C.gconv = din("gconv", [L, 128, 12 * 5])
    C.fconv = din("fconv", [L, 128, NFC * 4])
    C.small = din("small", [L, 8 + 8 + 8 + 256])
    C.pcol = din("pcol", [L, 128, 2])
    C.stripA = din("stripA", [128, 4 * SA_W])
    C.stripC = din("stripC", [128, 8 * 384])
    C.consts = din("consts", [128, 1024])
    C.out = nc.dram_tensor("out", [NS, T, D], F32, kind="ExternalOutput").ap()
    skind = "ExternalOutput" if dbg else "Internal"
    C.xT_d = [nc.dram_tensor(f"xT_d{i}", [128, KC * TP], BF16, kind=skind).ap() for i in range(2)]
    C.xres_d = [nc.dram_tensor(f"xres_d{i}", [T, D], F32, kind=skind).ap() for i in range(2)]
    C.yT_d = [nc.dram_tensor(f"yT_d{i}", [128, 4 * T], BF16, kind=skind).ap() for i in range(3)]

    C.cf = k.sb("cf", [128, 1024], F32)
    C.identb = k.sb("identb", [128, 128], BF16)
    C.onesb = k.sb("onesb", [128, 128], BF16)
    with k.phase():
        k.dma("sp", C.cf[:], C.consts[:, :], W=["cf"])
        k.op("dve", lambda e: e.tensor_copy(out=C.identb[:], in_=C.cf[:, C_ID:C_ID + 128]), R=["cf"], W=["identb"])
        k.op("dve", lambda e: e.tensor_copy(out=C.onesb[:], in_=C.cf[:, C_ONES:C_ONES + 128]), R=["cf"], W=["onesb"])
        z = k.sb("zt", [128, KC, 2], BF16)
        k.op("dve", lambda e: e.memset(z[:], 0.0), W=["zt"])
        for i in range(2):
            v3 = C.xT_d[i].rearrange("p (c t) -> p c t", c=KC)
            k.dma("sp", v3[:, :, 0:2], z[:], R=["zt"], W=[f"xTd{i}h"])
            k.dma("sp", v3[:, :, TP - 2:TP], z[:], R=["zt"], W=[f"xTd{i}h"])
    C.identf = C.cf[:, C_ID:C_ID + 128]

    for s in range(NS):
        for l in range(L):
            last = (l == L - 1)
            if l == 0 and "0" in phases:
                phase_p0(k, C, s)
            if "a" in phases:
                phase_a(k, C, s, l)
            if "b" in phases:
                phase_b(k, C, s, l)
            if "c" in phases:
                phase_c(k, C, s, l)
            if "m" in phases:
                phase_m(k, C, s, l)
            if "f" in phases:
                phase_f(k, C, s, l, last)
    return k.finish()


def wsrc(w2d, col0, n):
    return w2d[:, col0:col0 + n].rearrange("(c p) n -> p c n", p=128)


def xT_view(ap2d, TP):
    return ap2d.rearrange("p (c t) -> p c t", c=KC)


def transpose_store(k, C, src_bf, dst3, col0, psT, tag):
    for kc in range(KC):
        k.op("pe", lambda e, kc=kc: e.transpose(out=psT[:, kc, :], in_=src_bf[:, kc * 128:(kc + 1) * 128], identity=C.identb[:]),
             R=[tag + "src", "identb"], W=[tag + "psT"])
    k.op("act", lambda e: e.activation(out=dst3[:, :, col0:col0 + 128], in_=psT[:, :, :], func=AF.Copy),
         R=[tag + "psT"], W=[tag + "dst"])


def phase_p0(k, C, s):
    T, NT, TP = C.T, C.NT, C.TP
    with k.phase():
        xT = k.sb("p0_xT", [128, KC, TP], BF16)
        psT = k.ps("p0_psT", [128, KC, 128], BF16)
        k.op("pool", lambda e: e.memset(xT[:], 0.0), W=["p0dst"])
        for tt in range(NT):
            xs = k.sb(f"p0_xs{tt % 2}", [128, D], F32) if tt < 2 else xs_l[tt % 2]
            if tt < 2:
                if tt == 0:
                    xs_l = [None, None]
                    xb_l = [k.sb(f"p0_xb{i}", [128, D], BF16) for i in range(2)]
                xs_l[tt] = xs
            xb = xb_l[tt % 2]
            k.dma("sp", xs[:], C.x[s, tt * 128:(tt + 1) * 128, :], W=[f"p0xs{tt % 2}"])
            k.op("dve", lambda e, xs=xs, xb=xb: e.tensor_copy(out=xb[:], in_=xs[:]), R=[f"p0xs{tt % 2}"], W=["p0src"])
            transpose_store(k, C, xb, xT, 2 + tt * 128, psT, "p0")
        k.dma("sp", C.xT_d[0], xT[:].rearrange("p c t -> p (c t)"), R=["p0dst"], W=["xTd0"])


def load_small(k, C, l, names):
    sm = k.sb("small_sb", [128, 280], F32)
    k.dma("sp", sm[:], C.small[l:l + 1, :].partition_broadcast(128), W=["small_sb"])
    return sm


def phase_a(k, C, s, l):
    T, NT, QC, NQC, TP = C.T, C.NT, C.QC, C.NQC, C.TP
    NJ = QC // 128
    lam_init = 0.8 - 0.6 * math.exp(-0.3 * l)
    with k.phase():
        qkT = k.sb("a_qkT", [128, 8, T], BF16)
        vaug = k.sb("a_vaug", [128, NT, 4, 130], BF16)
        sm = load_small(k, C, l, None)
        pcol = k.sb("a_pcol", [128, 2], F32)
        k.dma("sp", pcol[:], C.pcol[l], W=["a_pcol"])
        psS = [k.ps(f"a_psS{i}", [128, 512], F32) for i in range(3)]
        with k.phase():
            xT = k.sb("a_xT", [128, KC, TP], BF16)
            wA = k.sb("a_wA", [128, KC, 1536], BF16)
            k.dma("sp", xT[:].rearrange("p c t -> p (c t)"), C.xT_d[0], R=["xTd0", "xTd0h"], W=["a_xT"])
            for j in range(3):
                k.dma("pool", wA[:, :, j * 512:(j + 1) * 512], wsrc(C.w_in[l], j * 512, 512), W=[f"a_wA{j}"])
            k.op("pool", lambda e: e.memset(vaug[:], 1.0), W=["a_vaug"])
            cnt = 0
            for h in range(4):
                for which in range(2):
                    col0 = which * 512 + h * 128
                    for tb in range(C.NTB):
                        ps = psS[cnt % 3]
                        key = f"a_psS{cnt % 3}"
                        cnt += 1
                        for kc in range(KC):
                            k.op("pe", lambda e, ps=ps, kc=kc, col0=col0, tb=tb: e.matmul(
                                ps[:, 0:C.TB], lhsT=wA[:, kc, col0:col0 + 128], rhs=xT[:, kc, 2 + tb * C.TB:2 + (tb + 1) * C.TB],
                                start=(kc == 0), stop=(kc == KC - 1)), R=["a_xT", f"a_wA{which}"], W=[key])
                        sc = 0.125 if which == 0 else 1.0
                        k.op("act", lambda e, ps=ps, h=h, which=which, tb=tb, sc=sc: e.activation(
                            out=qkT[:, which * 4 + h, tb * C.TB:(tb + 1) * C.TB], in_=ps[:, 0:C.TB], func=AF.Copy, scale=sc),
                            R=[key], W=[f"a_qkT{which * 4 + h}"])
            for tt in range(NT):
                ps = psS[cnt % 3]
                key = f"a_psS{cnt % 3}"
                cnt += 1
                for kc in range(KC):
                    k.op("pe", lambda e, ps=ps, kc=kc, tt=tt: e.matmul(
                        ps[:, :], lhsT=xT[:, kc, 2 + tt * 128:2 + (tt + 1) * 128], rhs=wA[:, kc, 1024:1536],
                        start=(kc == 0), stop=(kc == KC - 1)), R=["a_xT", "a_wA2"], W=[key])
                k.op("dve", lambda e, ps=ps, tt=tt: e.tensor_copy(
                    out=vaug[:, tt, :, 0:128], in_=ps[:, :].rearrange("p (h d) -> p h d", h=4)),
                    R=[key], W=["a_vaug"])
        strip = k.sb("a_strip", [128, 4, SA_W], BF16)
        k.dma("pool", strip[:], C.stripA.rearrange("p (h w) -> p h w", h=4), W=["a_strip"], max_dma_last_dim=2816)
        farb = k.sb("a_farb", [128, 4, 2], F32)
        k.op("dve", lambda e: e.tensor_copy(out=farb[:, :, 0:1], in_=strip[:, :, 0:1]), R=["a_strip"], W=["a_farb"])
        k.op("dve", lambda e: e.tensor_copy(out=farb[:, :, 1:2], in_=strip[:, :, SA_W - 1:SA_W]), R=["a_strip"], W=["a_farb"])
        lam = k.sb("a_lam", [128, 4], F32)
        junk = k.sb("a_junk", [128, 64], F32)
        LV = 24
        k.op("dve", lambda e: e.tensor_tensor(out=junk[:], in0=sm[:, LV:LV + 64], in1=sm[:, LV + 64:LV + 128], op=ALU.mult), R=["small_sb"], W=["a_junk"])
        k.op("dve", lambda e: e.reduce_sum(out=lam[:, 0:1], in_=junk[:], axis=AX.X), R=["a_junk"], W=["a_lam0"])
        k.op("dve", lambda e: e.tensor_tensor(out=junk[:], in0=sm[:, LV + 128:LV + 192], in1=sm[:, LV + 192:LV + 256], op=ALU.mult), R=["small_sb", "a_lam0"], W=["a_junk"])
        k.op("dve", lambda e: e.reduce_sum(out=lam[:, 1:2], in_=junk[:], axis=AX.X), R=["a_junk"], W=["a_lam1"])
        k.op("act", lambda e: e.activation(out=lam[:, 0:2], in_=lam[:, 0:2], func=AF.Exp), R=["a_lam0", "a_lam1"], W=["a_lam01"])
        k.op("dve", lambda e: e.scalar_tensor_tensor(out=lam[:, 2:3], in0=lam[:, 0:1], scalar=lam_init, in1=lam[:, 1:2], op0=ALU.add, op1=ALU.subtract), R=["a_lam01"], W=["a_lam"])
        subs = k.sb("a_subs", [128, 1], F32)
        k.op("dve", lambda e: e.tensor_scalar(out=subs[:], in0=pcol[:, 0:1], scalar1=(1.0 - lam_init), scalar2=None, op0=ALU.mult), R=["a_pcol"], W=["a_subs"])

        PT = [k.sb(f"a_PT{i}", [128, NT, QC], BF16) for i in range(2)]
        po = [k.ps(f"a_po{c}", [128, 4, 256], F32) for c in range(2)]
        psT = k.ps("a_psT", [128, 8, 128], BF16)
        yaT = k.sb("a_yaT", [128, 4, T], BF16)
        rr = k.sb("a_rr", [128, 2, 4], F32)
        ss = k.sb("a_ss", [128, 4], F32)
        tmp = [k.sb(f"a_tmp{j}", [128, 128], F32) for j in range(4)]
        osb = [k.sb(f"a_o{j}", [128, 128], F32) for j in range(4)]
        ybf = [k.sb(f"a_yb{j}", [128, 128], BF16) for j in range(4)]
        sq = k.sb("a_sq", [128, 128], F32)

        units = [(h, qc, c) for h in range(4) for qc in range(NQC) for c in range(2)]
        scnt = [0]

        def emit_qk(ui):
            h, qc, c = units[ui]
            buf = ui % 2
            pr = slice(64 * c, 64 * c + 64)
            for kt in range(NT):
                si = scnt[0] % 3
                scnt[0] += 1
                ps = psS[si]
                key = f"a_psS{si}"
                soff = qc * QC - kt * 128
                near = (soff < 255) and (soff + QC > -128)
                k.op("pe", lambda e, ps=ps, kt=kt, h=h, qc=qc, pr=pr, near=near: e.matmul(
                    ps[:, 0:QC], lhsT=qkT[pr, 4 + h, kt * 128:(kt + 1) * 128], rhs=qkT[pr, h, qc * QC:(qc + 1) * QC],
                    start=True, stop=(not near)), R=[f"a_qkT{h}", f"a_qkT{4 + h}"], W=[key])
                if near:
                    u0 = soff + 640
                    k.op("pe", lambda e, ps=ps, h=h, u0=u0: e.matmul(
                        ps[:, 0:QC], lhsT=C.identb[:], rhs=strip[:, h, u0:u0 + QC], start=False, stop=True),
                        R=["identb", "a_strip"], W=[key])
                    k.op("act", lambda e, ps=ps, buf=buf, kt=kt: e.activation(
                        out=PT[buf][:, kt, :], in_=ps[:, 0:QC], func=AF.Exp), R=[key], W=[f"a_PT{buf}_{kt}"])
                else:
                    fi = 0 if soff < 0 else 1
                    k.op("act", lambda e, ps=ps, buf=buf, kt=kt, h=h, fi=fi: e.activation(
                        out=PT[buf][:, kt, :], in_=ps[:, 0:QC], func=AF.Exp, bias=farb[:, h, fi:fi + 1]),
                        R=[key, "a_farb"], W=[f"a_PT{buf}_{kt}"])

        def emit_pv(ui):
            h, qc, c = units[ui]
            buf = ui % 2
            for j in range(NJ):
                for kt in range(NT):
                    k.op("pe", lambda e, j=j, kt=kt, c=c, h=h, buf=buf: e.matmul(
                        po[c][:, j, 0:129], lhsT=PT[buf][:, kt, j * 128:(j + 1) * 128], rhs=vaug[:, kt, h, 0:129],
                        start=(kt == 0), stop=(kt == NT - 1)), R=[f"a_PT{buf}_{kt}", "a_vaug"], W=[f"a_po{c}"])
            if c == 1:
                for cc in range(2):
                    k.op("dve", lambda e, cc=cc: e.reciprocal(out=rr[:, cc, 0:NJ], in_=po[cc][:, 0:NJ, 128]), R=[f"a_po{cc}"], W=[f"a_rr{cc}"])
                k.op("dve", lambda e: e.tensor_scalar(out=rr[:, 1, 0:NJ], in0=rr[:, 1, 0:NJ], scalar1=lam[:, 2:3], scalar2=None, op0=ALU.mult), R=["a_rr1", "a_lam"], W=["a_rr1"])
                for j in range(NJ):
                    k.op("act", lambda e, j=j: e.activation(out=tmp[j][:], in_=po[1][:, j, 0:128], func=AF.Copy, scale=rr[:, 1, j:j + 1]),
                         R=["a_po1", "a_rr1"], W=[f"a_tmp{j}"])
                    k.op("dve", lambda e, j=j: e.scalar_tensor_tensor(out=osb[j][:], in0=po[0][:, j, 0:128], scalar=rr[:, 0, j:j + 1], in1=tmp[j][:],
                                                                   op0=ALU.mult, op1=ALU.subtract), R=["a_po0", "a_rr0", f"a_tmp{j}"], W=[f"a_o{j}"])
                    k.op("act", lambda e, j=j: e.activation(out=sq[:], in_=osb[j][:], func=AF.Square, accum_out=ss[:, j:j + 1]),
                         R=[f"a_o{j}"], W=["a_sq", f"a_ss{j}"])
                ssk = [f"a_ss{j}" for j in range(NJ)]
                k.op("dve", lambda e: e.tensor_scalar(out=ss[:, 0:NJ], in0=ss[:, 0:NJ], scalar1=1.0 / 128, scalar2=RMS_EPS, op0=ALU.mult, op1=ALU.add), R=ssk, W=["a_ssa"])
                k.op("act", lambda e: e.activation(out=ss[:, 0:NJ], in_=ss[:, 0:NJ], func=AF.Sqrt), R=["a_ssa"], W=["a_ssb"])
                k.op("dve", lambda e: e.reciprocal(out=ss[:, 0:NJ], in_=ss[:, 0:NJ]), R=["a_ssb"], W=["a_ssc"] + ssk)
                for j in range(NJ):
                    k.op("dve", lambda e, j=j: e.tensor_scalar(out=ybf[j][:], in0=osb[j][:], scalar1=ss[:, j:j + 1], scalar2=None, op0=ALU.mult),
                         R=[f"a_o{j}", "a_ssc"], W=[f"a_yb{j}"])
                    k.op("pe", lambda e, j=j: e.transpose(out=psT[:, j, :], in_=ybf[j][:], identity=C.identb[:]), R=[f"a_yb{j}", "identb"], W=["a_psT"])
                k.op("act", lambda e, h=h, qc=qc: e.activation(out=yaT[:, h, qc * QC:(qc + 1) * QC], in_=psT[:, 0:NJ, :].rearrange("p j d -> p (j d)"),
                                                            func=AF.Copy, scale=subs[:, 0:1]), R=["a_psT", "a_subs"], W=["a_yaT"])

        for ui in range(len(units) + 1):
            if ui < len(units):
                emit_qk(ui)
            if ui >= 1:
                emit_pv(ui - 1)
        k.dma("sp", C.yT_d[0], yaT[:].rearrange("p h t -> p (h t)"), R=["a_yaT"], W=["yTd0"])


def phase_b(k, C, s, l):
    T, NT, TP, TB, NTB = C.T, C.NT, C.TP, C.TB, C.NTB
    NCH = NT
    cf = C.cf
    identf = cf[:, C_ID:C_ID + 128]
    with k.phase():
        qT = k.sb("b_qT", [128, 4, T], BF16)
        kT = k.sb("b_kT", [128, 4, T], BF16)
        ktok = k.sb("b_ktok", [128, NCH, 4, 128], BF16)
        vtok = k.sb("b_vtok", [128, NCH, 4, 128], BF16)
        sbt = k.sb("b_sbt", [128, NCH, 8], F32)
        gc = k.sb("b_gc", [128, NCH, 8], F32)
        egc = k.sb("b_egc", [128, NCH, 8], F32)
        egl = k.sb("b_egl", [128, NCH, 8], F32)
        nsb = k.sb("b_nsb", [128, NCH, 8], F32)
        nsed = k.sb("b_nsed", [128, NCH, 8], F32)
        sm = load_small(k, C, l, None)
        pcol = k.sb("b_pcol", [128, 2], F32)
        k.dma("sp", pcol[:], C.pcol[l], W=["b_pcol"])
        with k.phase():
            xT = k.sb("b_xT", [128, KC, TP], BF16)
            k.dma("sp", xT[:].rearrange("p c t -> p (c t)"), C.xT_d[0], R=["xTd0", "xTd0h"], W=["b_xT"])
            gcv = k.sb("b_gcv", [128, 12, 5], F32)
            k.dma("sp", gcv[:].rearrange("p a b -> p (a b)"), C.gconv[l], W=["b_gcv"])
            wq = [k.sb(f"b_wq{i}", [128, KC, 128], BF16) for i in range(2)]
            wab = k.sb("b_wab", [128, KC, 16], BF16)
            k.dma("pool", wab[:], wsrc(C.w_in[l], BA, 16), W=["b_wab"])
            pre = [k.sb(f"b_pre{i}", [128, TP], F32) for i in range(2)]
            for i in range(2):
                k.op("pool", lambda e, i=i: e.memset(pre[i][:], 0.0), W=[f"b_pre{i}"])
            acc = k.sb("b_acc", [128, T], F32)
            hs = k.sb("b_hs", [128, T], F32)
            sq = k.sb("b_sq", [128, T], BF16)
            vT = k.sb("b_vT", [128, T], BF16)
            rs = [k.sb(f"b_rs{i}", [128, TB], F32) for i in range(2)]
            psP = [k.ps(f"b_psP{i}", [128, 512], F32) for i in range(2)]
            psS = [k.ps(f"b_psS{i}", [128, 512], F32) for i in range(2)]
            psTb_ = [k.ps(f"b_psTb{i}", [128, 8, 128], BF16) for i in range(2)]
            psTb = [p_[:, 0, :] for p_ in psTb_]
            psab_ = k.ps("b_psab", [128, 512], F32)
            psab = psab_[:, 0:NT * 16].rearrange("p (t c) -> p t c", c=16)
            psg_ = k.ps("b_psg", [128, 512], F32)
            psg = psg_[:, 0:NCH * 16].rearrange("p (t c) -> p t c", c=16)
            cnt = 0
            tcnt = 0
            for cc in range(12):
                b = cc % 2
                kind, h = cc // 4, cc % 4
                k.dma("pool", wq[b][:], wsrc(C.w_in[l], BQKV + cc * 128, 128), W=[f"b_wq{b}"])
                pr = pre[b]
                for tb in range(NTB):
                    ps = psP[cnt % 2]
                    key = f"b_psP{cnt % 2}"
                    cnt += 1
                    for kc in range(KC):
                        k.op("pe", lambda e, ps=ps, kc=kc, b=b, tb=tb: e.matmul(
                            ps[:, 0:TB], lhsT=wq[b][:, kc, :], rhs=xT[:, kc, 2 + tb * TB:2 + (tb + 1) * TB],
                            start=(kc == 0), stop=(kc == KC - 1)), R=["b_xT", f"b_wq{b}"], W=[key])
                    k.op("act", lambda e, ps=ps, pr=pr, tb=tb: e.activation(out=pr[:, 2 + tb * TB:2 + (tb + 1) * TB], in_=ps[:, 0:TB], func=AF.Copy),
                         R=[key], W=[f"b_pre{b}"])
                k.op("dve", lambda e, pr=pr, cc=cc: e.tensor_scalar(out=acc[:], in0=pr[:, 0:T], scalar1=gcv[:, cc, 0:1], scalar2=None, op0=ALU.mult),
                     R=[f"b_pre{b}", "b_gcv"], W=["b_acc"])
                for j in range(1, 5):
                    k.op("dve", lambda e, pr=pr, cc=cc, j=j: e.scalar_tensor_tensor(out=acc[:], in0=pr[:, j:j + T], scalar=gcv[:, cc, j:j + 1], in1=acc[:],
                                                                              op0=ALU.mult, op1=ALU.add), R=[f"b_pre{b}", "b_gcv", "b_acc"], W=["b_acc"])
                if kind == 2:
                    k.op("act", lambda e: e.activation(out=vT[:], in_=acc[:], func=AF.Silu), R=["b_acc"], W=["b_vT"])
                    for n in range(NCH):
                        pt = psTb[tcnt % 2]
                        kt_ = f"b_psTb{tcnt % 2}"
                        tcnt += 1
                        k.op("pe", lambda e, pt=pt, n=n: e.transpose(out=pt, in_=vT[:, n * 128:(n + 1) * 128], identity=C.identb[:]), R=["b_vT", "identb"], W=[kt_])
                        k.op("act", lambda e, pt=pt, n=n, h=h: e.activation(out=vtok[:, n, h, :], in_=pt, func=AF.Copy), R=[kt_], W=["b_vtok"])
                else:
                    k.op("act", lambda e: e.activation(out=hs[:], in_=acc[:], func=AF.Silu), R=["b_acc"], W=["b_hs"])
                    k.op("pool", lambda e: e.tensor_tensor(out=sq[:], in0=hs[:], in1=hs[:], op=ALU.mult), R=["b_hs"], W=["b_sq"])
                    dst = qT if kind == 0 else kT
                    dkey = "b_qT" if kind == 0 else "b_kT"
                    for tb in range(NTB):
                        ps = psS[tb % 2]
                        key = f"b_psS{tb % 2}"
                        r_ = rs[tb % 2]
                        rk = f"b_rs{tb % 2}"
                        k.op("pe", lambda e, ps=ps, tb=tb: e.matmul(ps[:, 0:TB], lhsT=C.onesb[:], rhs=sq[:, tb * TB:(tb + 1) * TB], start=True, stop=True),
                             R=["b_sq", "onesb"], W=[key])
                        k.op("dve", lambda e, ps=ps, r_=r_: e.tensor_scalar(out=r_[:], in0=ps[:, 0:TB], scalar1=RMS_EPS, scalar2=None, op0=ALU.add), R=[key], W=[rk])
                        k.op("act", lambda e, r_=r_: e.activation(out=r_[:], in_=r_[:], func=AF.Sqrt), R=[rk], W=[rk])
                        k.op("dve", lambda e, r_=r_: e.reciprocal(out=r_[:], in_=r_[:]), R=[rk], W=[rk])
                        scl = (128.0 ** -0.5) if kind == 0 else 1.0
                        k.op("dve", lambda e, r_=r_, tb=tb, dst=dst, h=h, scl=scl: e.scalar_tensor_tensor(
                            out=dst[:, h, tb * TB:(tb + 1) * TB], in0=hs[:, tb * TB:(tb + 1) * TB], scalar=scl, in1=r_[:], op0=ALU.mult, op1=ALU.mult),
                            R=["b_hs", rk], W=[dkey])
                    if kind == 1:
                        for n in range(NCH):
                            pt = psTb[tcnt % 2]
                            kt_ = f"b_psTb{tcnt % 2}"
                            tcnt += 1
                            k.op("pe", lambda e, pt=pt, n=n, h=h: e.transpose(out=pt, in_=kT[:, h, n * 128:(n + 1) * 128], identity=C.identb[:]), R=["b_kT", "identb"], W=[kt_])
                            k.op("act", lambda e, pt=pt, n=n, h=h: e.activation(out=ktok[:, n, h, :], in_=pt, func=AF.Copy), R=[kt_], W=["b_ktok"])
            for tt in range(NT):
                for kc in range(KC):
                    k.op("pe", lambda e, kc=kc, tt=tt: e.matmul(psab[:, tt, :], lhsT=xT[:, kc, 2 + tt * 128:2 + (tt + 1) * 128], rhs=wab[:, kc, :],
                                                               start=(kc == 0), stop=(kc == KC - 1)), R=["b_xT", "b_wab"], W=["b_psab"])
            xa = k.sb("b_xa", [128, NT, 8], F32)
            gg = k.sb("b_gg", [128, NT, 8], F32)
            negA = k.sb("b_negA", [128, 8], F32)
            k.op("dve", lambda e: e.tensor_tensor(out=xa[:], in0=psab[:, :, 0:8], in1=sm[:, 8:16].unsqueeze(1).broadcast_to([128, NT, 8]), op=ALU.add),
                 R=["b_psab", "small_sb"], W=["b_xa"])
            k.op("act", lambda e: e.activation(out=xa[:], in_=xa[:], func=AF.Exp), R=["b_xa"], W=["b_xa"])
            k.op("act", lambda e: e.activation(out=xa[:], in_=xa[:], func=AF.Ln, bias=1.0), R=["b_xa"], W=["b_xa"])
            k.op("act", lambda e: e.activation(out=negA[:], in_=sm[:, 0:8], func=AF.Exp), R=["small_sb"], W=["b_negA"])
            k.op("dve", lambda e: e.tensor_scalar(out=negA[:], in0=negA[:], scalar1=-1.0, scalar2=None, op0=ALU.mult), R=["b_negA"], W=["b_negA"])
            k.op("dve", lambda e: e.tensor_tensor(out=gg[:], in0=xa[:], in1=negA[:, :].unsqueeze(1).broadcast_to([128, NT, 8]), op=ALU.mult),
                 R=["b_xa", "b_negA"], W=["b_gg"])
            k.op("act", lambda e: e.activation(out=sbt[:], in_=psab[:, :, 8:16], func=AF.Sigmoid), R=["b_psab"], W=["b_sbt"])
            k.op("act", lambda e: e.activation(out=sbt[:], in_=sbt[:], func=AF.Sqrt), R=["b_sbt"], W=["b_sbt"])
            for n in range(NCH):
                k.op("pe", lambda e, n=n: e.matmul(psg[:, n, 0:4], lhsT=cf[:, C_TRIF:C_TRIF + 128], rhs=gg[:, n, 0:4], start=True, stop=True), R=["b_gg", "cf"], W=["b_psg"])
                k.op("pe", lambda e, n=n: e.matmul(psg[:, n, 4:8], lhsT=cf[:, C_TRIB:C_TRIB + 128], rhs=gg[:, n, 4:8], start=True, stop=True), R=["b_gg", "cf"], W=["b_psg"])
                k.op("pe", lambda e, n=n: e.matmul(psg[:, n, 8:16], lhsT=cf[:, C_ONES:C_ONES + 128], rhs=gg[:, n, 0:8], start=True, stop=True), R=["b_gg", "cf"], W=["b_psg"])
            k.op("dve", lambda e: e.tensor_copy(out=gc[:], in_=psg[:, :, 0:8]), R=["b_psg"], W=["b_gc"])
            k.op("act", lambda e: e.activation(out=egc[:], in_=psg[:, :, 0:8], func=AF.Exp), R=["b_psg"], W=["b_egc"])
            k.op("act", lambda e: e.activation(out=egl[:], in_=psg[:, :, 8:16], func=AF.Exp), R=["b_psg"], W=["b_egl"])
            k.op("dve", lambda e: e.tensor_tensor(out=nsed[:], in0=psg[:, :, 8:16], in1=gc[:], op=ALU.subtract), R=["b_psg", "b_gc"], W=["b_nsed"])
            k.op("act", lambda e: e.activation(out=nsed[:], in_=nsed[:], func=AF.Exp), R=["b_nsed"], W=["b_nsed"])
            k.op("dve", lambda e: e.tensor_scalar(out=nsb[:], in0=sbt[:], scalar1=-1.0, scalar2=None, op0=ALU.mult), R=["b_sbt"], W=["b_nsb"])
            k.op("dve", lambda e: e.tensor_tensor(out=nsed[:], in0=nsed[:], in1=nsb[:], op=ALU.mult), R=["b_nsed", "b_nsb"], W=["b_nsed"])
        oacc = k.sb("b_oacc", [128, NCH, 4, 128], F32)
        with k.phase():
            psb = [k.ps(f"b_pb{i}", [128, 4, 128], F32) for i in range(8)]
            slot = [0]

            def pslot():
                i = slot[0] % 32
                slot[0] += 1
                return psb[i // 4][:, i % 4, :], f"b_pb{i // 4}"

            CH = []
            for ci in range(8):
                t_ = {}
                for nm in ("aDT", "aD", "Na", "Nb", "NTa", "NTb", "Ra", "Rb", "S"):
                    t_[nm] = k.sb(f"b_{nm}{ci}", [128, 128], F32)
                for nm in ("ktil", "vb2", "QKT", "Rbf", "T1n", "vnew", "vnew2", "Sbf"):
                    t_[nm] = k.sb(f"b_{nm}{ci}", [128, 128], BF16)
                CH.append(t_)
            visited = set()
            for t in range(NCH):
                chains = []
                for d in range(2):
                    n = t if d == 0 else NCH - 1 - t
                    for h in range(4):
                        chains.append((d * 4 + h, d, h, n))
                P = {}
                K_ = lambda ci, nm: f"b_{nm}{ci}"
                for ci, d, h, n in chains:
                    P[ci, "sb"] = pslot()
                    P[ci, "gc"] = pslot()
                    k.op("pe", lambda e, o=P[ci, "sb"][0], n=n, ci=ci: e.matmul(o, lhsT=sbt[:, n, ci:ci + 1].broadcast_to([128, 128]), rhs=identf, start=True, stop=True),
                         R=["b_sbt", "cf"], W=[P[ci, "sb"][1]])
                    k.op("pe", lambda e, o=P[ci, "gc"][0], n=n, ci=ci: e.matmul(o, lhsT=gc[:, n, ci:ci + 1].broadcast_to([128, 128]), rhs=identf, start=True, stop=True),
                         R=["b_gc", "cf"], W=[P[ci, "gc"][1]])
                for ci, d, h, n in chains:
                    c = CH[ci]
                    mA = cf[:, (C_MAF if d == 0 else C_MAB):(C_MAF if d == 0 else C_MAB) + 128]
                    mD = cf[:, (C_MDF if d == 0 else C_MDB):(C_MDF if d == 0 else C_MDB) + 128]
                    k.op("dve", lambda e, c=c, o=P[ci, "sb"][0], n=n, h=h: e.tensor_tensor(out=c["ktil"][:], in0=o, in1=kT[:, h, n * 128:(n + 1) * 128], op=ALU.mult),
                         R=[P[ci, "sb"][1], "b_kT"], W=[K_(ci, "ktil")])
                    k.op("dve", lambda e, c=c, o=P[ci, "gc"][0], n=n, ci=ci, mA=mA: e.scalar_tensor_tensor(out=c["aDT"][:], in0=o, scalar=gc[:, n, ci:ci + 1], in1=mA, op0=ALU.subtract, op1=ALU.add),
                         R=[P[ci, "gc"][1], "b_gc", "cf"], W=[K_(ci, "aDT")])
                    k.op("dve", lambda e, c=c, o=P[ci, "gc"][0], n=n, ci=ci, mD=mD: e.scalar_tensor_tensor(out=c["aD"][:], in0=o, scalar=gc[:, n, ci:ci + 1], in1=mD, op0=ALU.subtract, op1=ALU.add),
                         R=[P[ci, "gc"][1], "b_gc", "cf"], W=[K_(ci, "aD")])
                    k.op("pool", lambda e, c=c, n=n, h=h, ci=ci: e.tensor_scalar(out=c["vb2"][:], in0=vtok[:, n, h, :], scalar1=sbt[:, n, ci:ci + 1], scalar2=None, op0=ALU.mult),
                         R=["b_vtok", "b_sbt"], W=[K_(ci, "vb2")])
                for ci, d, h, n in chains:
                    c = CH[ci]
                    k.op("act", lambda e, c=c: e.activation(out=c["aDT"][:], in_=c["aDT"][:], func=AF.Exp), R=[K_(ci, "aDT")], W=[K_(ci, "aDT")])
                    k.op("act", lambda e, c=c: e.activation(out=c["aD"][:], in_=c["aD"][:], func=AF.Exp, scale=-1.0), R=[K_(ci, "aD")], W=[K_(ci, "aD")])
                for ci, d, h, n in chains:
                    c = CH[ci]
                    P[ci, "G"] = pslot()
                    P[ci, "GQ"] = pslot()
                    k.op("pe", lambda e, c=c, o=P[ci, "G"][0]: e.matmul(o, lhsT=c["ktil"][:], rhs=c["ktil"][:], start=True, stop=True), R=[K_(ci, "ktil")], W=[P[ci, "G"][1]])
                    k.op("pe", lambda e, o=P[ci, "GQ"][0], n=n, h=h: e.matmul(o, lhsT=kT[:, h, n * 128:(n + 1) * 128], rhs=qT[:, h, n * 128:(n + 1) * 128], start=True, stop=True),
                         R=["b_kT", "b_qT"], W=[P[ci, "GQ"][1]])
                for ci, d, h, n in chains:
                    c = CH[ci]
                    k.op("dve", lambda e, c=c, o=P[ci, "G"][0]: e.scalar_tensor_tensor(out=c["NTa"][:], in0=o, scalar=-1.0, in1=c["aD"][:], op0=ALU.mult, op1=ALU.mult),
                         R=[P[ci, "G"][1], K_(ci, "aD")], W=[K_(ci, "NTa")])
                    k.op("dve", lambda e, c=c, o=P[ci, "GQ"][0]: e.tensor_tensor(out=c["QKT"][:], in0=o, in1=c["aDT"][:], op=ALU.mult),
                         R=[P[ci, "GQ"][1], K_(ci, "aDT")], W=[K_(ci, "QKT")])
                for ci, d, h, n in chains:
                    c = CH[ci]
                    P[ci, "T"] = pslot()
                    k.op("pe", lambda e, c=c, o=P[ci, "T"][0]: e.transpose(out=o, in_=c["NTa"][:], identity=identf), R=[K_(ci, "NTa"), "cf"], W=[P[ci, "T"][1]])
                for ci, d, h, n in chains:
                    c = CH[ci]
                    k.op("act", lambda e, c=c, o=P[ci, "T"][0]: e.activation(out=c["Na"][:], in_=o, func=AF.Copy), R=[P[ci, "T"][1]], W=[K_(ci, "Na")])
                    k.op("pool", lambda e, c=c: e.tensor_tensor(out=c["Ra"][:], in0=c["Na"][:], in1=identf, op=ALU.add), R=[K_(ci, "Na"), "cf"], W=[K_(ci, "Ra")])
                cur = {ci: ("Na", "NTa", "Ra") for ci, _, _, _ in chains}
                for j in range(1, 7):
                    for ci, d, h, n in chains:
                        c = CH[ci]
                        Nc, NTc, Rc = cur[ci]
                        if j <= 5:
                            P[ci, "pN"] = pslot()
                            k.op("pe", lambda e, c=c, o=P[ci, "pN"][0], Nc=Nc, NTc=NTc: e.matmul(o, lhsT=c[NTc][:], rhs=c[Nc][:], start=True, stop=True),
                                 R=[K_(ci, Nc), K_(ci, NTc)], W=[P[ci, "pN"][1]])
                        P[ci, "pNT"] = pslot()
                        k.op("pe", lambda e, c=c, o=P[ci, "pNT"][0], Nc=Nc, NTc=NTc: e.matmul(o, lhsT=c[Nc][:], rhs=c[NTc][:], start=True, stop=True),
                             R=[K_(ci, Nc), K_(ci, NTc)], W=[P[ci, "pNT"][1]])
                    for ci, d, h, n in chains:
                        c = CH[ci]
                        Nc, NTc, Rc = cur[ci]
                        Nn = "Nb" if Nc == "Na" else "Na"
                        NTn = "NTb" if NTc == "NTa" else "NTa"
                        if j <= 5:
                            k.op("act", lambda e, c=c, o=P[ci, "pN"][0], Nn=Nn: e.activation(out=c[Nn][:], in_=o, func=AF.Copy), R=[P[ci, "pN"][1]], W=[K_(ci, Nn)])
                        k.op("act", lambda e, c=c, o=P[ci, "pNT"][0], NTn=NTn: e.activation(out=c[NTn][:], in_=o, func=AF.Copy), R=[P[ci, "pNT"][1]], W=[K_(ci, NTn)])
                        cur[ci] = (Nn, NTn, Rc)
                    for ci, d, h, n in chains:
                        c = CH[ci]
                        Nc, NTc, Rc = cur[ci]
                        P[ci, "pR"] = pslot()
                        k.op("pe", lambda e, c=c, o=P[ci, "pR"][0], NTc=NTc, Rc=Rc: e.matmul(o, lhsT=c[NTc][:], rhs=c[Rc][:], start=True, stop=True),
                             R=[K_(ci, NTc), K_(ci, Rc)], W=[P[ci, "pR"][1]])
                    for ci, d, h, n in chains:
                        c = CH[ci]
                        Nc, NTc, Rc = cur[ci]
                        Rn = ("Rb" if Rc == "Ra" else "Ra") if j < 6 else "Rbf"
                        k.op("dve", lambda e, c=c, o=P[ci, "pR"][0], Rc=Rc, Rn=Rn: e.tensor_tensor(out=c[Rn][:], in0=o, in1=c[Rc][:], op=ALU.add),
                             R=[P[ci, "pR"][1], K_(ci, Rc)], W=[K_(ci, Rn)])
                        cur[ci] = (Nc, NTc, Rn)
                first = (t == 0)
                if not first:
                    for ci, d, h, n in chains:
                        c = CH[ci]
                        P[ci, "P1"] = pslot()
                        P[ci, "P3"] = pslot()
                        k.op("pe", lambda e, c=c, o=P[ci, "P1"][0]: e.matmul(o, lhsT=c["ktil"][:], rhs=c["Sbf"][:], start=True, stop=True), R=[K_(ci, "ktil"), K_(ci, "Sbf")], W=[P[ci, "P1"][1]])
                        k.op("pe", lambda e, c=c, o=P[ci, "P3"][0], n=n, h=h: e.matmul(o, lhsT=qT[:, h, n * 128:(n + 1) * 128], rhs=c["Sbf"][:], start=True, stop=True),
                             R=["b_qT", K_(ci, "Sbf")], W=[P[ci, "P3"][1]])
                for ci, d, h, n in chains:
                    c = CH[ci]
                    okey = f"b_oacc{n}_{h}"
                    if first:
                        k.op("pool", lambda e, c=c: e.tensor_scalar(out=c["T1n"][:], in0=c["vb2"][:], scalar1=-1.0, scalar2=None, op0=ALU.mult), R=[K_(ci, "vb2")], W=[K_(ci, "T1n")])
                    else:
                        k.op("dve", lambda e, c=c, o=P[ci, "P1"][0], n=n, ci=ci: e.scalar_tensor_tensor(out=c["T1n"][:], in0=o, scalar=egc[:, n, ci:ci + 1], in1=c["vb2"][:], op0=ALU.mult, op1=ALU.subtract),
                             R=[P[ci, "P1"][1], "b_egc", K_(ci, "vb2")], W=[K_(ci, "T1n")])
                        if (n, h) in visited:
                            k.op("dve", lambda e, o=P[ci, "P3"][0], n=n, h=h, ci=ci: e.scalar_tensor_tensor(out=oacc[:, n, h, :], in0=o, scalar=egc[:, n, ci:ci + 1], in1=oacc[:, n, h, :], op0=ALU.mult, op1=ALU.add),
                                 R=[P[ci, "P3"][1], "b_egc", okey], W=[okey])
                        else:
                            k.op("dve", lambda e, o=P[ci, "P3"][0], n=n, h=h, ci=ci: e.tensor_scalar(out=oacc[:, n, h, :], in0=o, scalar1=egc[:, n, ci:ci + 1], scalar2=None, op0=ALU.mult),
                                 R=[P[ci, "P3"][1], "b_egc"], W=[okey])
                            visited.add((n, h))
                for ci, d, h, n in chains:
                    c = CH[ci]
                    P[ci, "P2"] = pslot()
                    k.op("pe", lambda e, c=c, o=P[ci, "P2"][0]: e.matmul(o, lhsT=c["Rbf"][:], rhs=c["T1n"][:], start=True, stop=True), R=[K_(ci, "Rbf"), K_(ci, "T1n")], W=[P[ci, "P2"][1]])
                for ci, d, h, n in chains:
                    c = CH[ci]
                    k.op("act", lambda e, c=c, o=P[ci, "P2"][0], n=n, ci=ci: e.activation(out=c["vnew"][:], in_=o, func=AF.Copy, scale=nsb[:, n, ci:ci + 1]), R=[P[ci, "P2"][1], "b_nsb"], W=[K_(ci, "vnew")])
                    k.op("act", lambda e, c=c, o=P[ci, "P2"][0], n=n, ci=ci: e.activation(out=c["vnew2"][:], in_=o, func=AF.Copy, scale=nsed[:, n, ci:ci + 1]), R=[P[ci, "P2"][1], "b_nsed"], W=[K_(ci, "vnew2")])
                for ci, d, h, n in chains:
                    c = CH[ci]
                    P[ci, "P4"] = pslot()
                    P[ci, "P5"] = pslot()
                    k.op("pe", lambda e, c=c, o=P[ci, "P4"][0]: e.matmul(o, lhsT=c["QKT"][:], rhs=c["vnew"][:], start=True, stop=True), R=[K_(ci, "QKT"), K_(ci, "vnew")], W=[P[ci, "P4"][1]])
                    k.op("pe", lambda e, c=c, o=P[ci, "P5"][0], n=n, h=h: e.matmul(o, lhsT=ktok[:, n, h, :], rhs=c["vnew2"][:], start=True, stop=True), R=["b_ktok", K_(ci, "vnew2")], W=[P[ci, "P5"][1]])
                for ci, d, h, n in chains:
                    c = CH[ci]
                    okey = f"b_oacc{n}_{h}"
                    if (n, h) in visited:
                        k.op("dve", lambda e, o=P[ci, "P4"][0], n=n, h=h: e.tensor_tensor(out=oacc[:, n, h, :], in0=o, in1=oacc[:, n, h, :], op=ALU.add), R=[P[ci, "P4"][1], okey], W=[okey])
                    else:
                        k.op("dve", lambda e, o=P[ci, "P4"][0], n=n, h=h: e.tensor_copy(out=oacc[:, n, h, :], in_=o), R=[P[ci, "P4"][1]], W=[okey])
                        visited.add((n, h))
                    if first:
                        k.op("dve", lambda e, c=c, o=P[ci, "P5"][0]: e.tensor_copy(out=c["S"][:], in_=o), R=[P[ci, "P5"][1]], W=[K_(ci, "S")])
                    else:
                        k.op("dve", lambda e, c=c, o=P[ci, "P5"][0], n=n, ci=ci: e.scalar_tensor_tensor(out=c["S"][:], in0=c["S"][:], scalar=egl[:, n, ci:ci + 1], in1=o, op0=ALU.mult, op1=ALU.add),
                             R=[P[ci, "P5"][1], "b_egl", K_(ci, "S")], W=[K_(ci, "S")])
                    k.op("act", lambda e, c=c: e.activation(out=c["Sbf"][:], in_=c["S"][:], func=AF.Copy), R=[K_(ci, "S")], W=[K_(ci, "Sbf")])
        with k.phase():
            xT = k.sb("b3_xT", [128, KC, TP], BF16)
            k.dma("sp", xT[:].rearrange("p c t -> p (c t)"), C.xT_d[0], R=["xTd0", "xTd0h"], W=["b3_xT"])
            wgt = k.sb("b3_wg", [128, KC, 512], BF16)
            k.dma("pool", wgt[:], wsrc(C.w_in[l], BG, 512), W=["b3_wg"])
            ybT = k.sb("b3_ybT", [128, 4, T], BF16)
            psG = [k.ps(f"b3_psG{i}", [128, 512], F32) for i in range(2)]
            psT = [k.ps(f"b3_psT{i}", [128, 8, 128], BF16) for i in range(2)]
            sg = [k.sb(f"b3_sg{i}", [128, 512], F32) for i in range(2)]
            ss = [k.sb(f"b3_ss{i}", [128, 4], F32) for i in range(2)]
            junk = k.sb("b3_junk", [128, 128], F32)
            yb = [k.sb(f"b3_yb{i}", [128, 4, 128], BF16) for i in range(2)]
            okeys_all = lambda n: [f"b_oacc{n}_{h}" for h in range(4)]
            for n in range(NCH):
                b = n % 2
                for kc in range(KC):
                    k.op("pe", lambda e, b=b, kc=kc, n=n: e.matmul(psG[b][:, :], lhsT=xT[:, kc, 2 + n * 128:2 + (n + 1) * 128], rhs=wgt[:, kc, :],
                                                                 start=(kc == 0), stop=(kc == KC - 1)), R=["b3_xT", "b3_wg"], W=[f"b3_psG{b}"])
                k.op("act", lambda e, b=b: e.activation(out=sg[b][:], in_=psG[b][:, :], func=AF.Silu), R=[f"b3_psG{b}"], W=[f"b3_sg{b}"])
                for h in range(4):
                    k.op("act", lambda e, n=n, h=h, b=b: e.activation(out=junk[:], in_=oacc[:, n, h, :], func=AF.Square, accum_out=ss[b][:, h:h + 1]),
                         R=okeys_all(n), W=["b3_junk", f"b3_ss{b}_{h}"])
                ssk = [f"b3_ss{b}_{h}" for h in range(4)]
                k.op("dve", lambda e, b=b: e.tensor_scalar(out=ss[b][:], in0=ss[b][:], scalar1=1.0 / 128, scalar2=RMS_EPS, op0=ALU.mult, op1=ALU.add), R=ssk, W=[f"b3_ssa{b}"])
                k.op("act", lambda e, b=b: e.activation(out=ss[b][:], in_=ss[b][:], func=AF.Sqrt), R=[f"b3_ssa{b}"], W=[f"b3_ssb{b}"])
                k.op("dve", lambda e, b=b: e.reciprocal(out=ss[b][:], in_=ss[b][:]), R=[f"b3_ssb{b}"], W=[f"b3_ssc{b}"] + ssk)
                for h in range(4):
                    k.op("dve", lambda e, n=n, h=h, b=b: e.scalar_tensor_tensor(out=yb[b][:, h, :], in0=oacc[:, n, h, :], scalar=ss[b][:, h:h + 1], in1=sg[b][:, h * 128:(h + 1) * 128],
                                                                             op0=ALU.mult, op1=ALU.mult), R=okeys_all(n) + [f"b3_ssc{b}", f"b3_sg{b}"], W=[f"b3_yb{b}_{h}"])
                    k.op("pe", lambda e, h=h, b=b: e.transpose(out=psT[b][:, h, :], in_=yb[b][:, h, :], identity=C.identb[:]), R=[f"b3_yb{b}_{h}", "identb"], W=[f"b3_psT{b}"])
                k.op("act", lambda e, n=n, b=b: e.activation(out=ybT[:, :, n * 128:(n + 1) * 128], in_=psT[b][:, 0:4, :], func=AF.Copy, scale=pcol[:, 1:2]),
                     R=[f"b3_psT{b}", "b_pcol"], W=["b3_ybT"])
            k.dma("sp", C.yT_d[1], ybT[:].rearrange("p h t -> p (h t)"), R=["b3_ybT"], W=["yTd1"])


def phase_c(k, C, s, l):
    T, NT, TP = C.T, C.NT, C.TP
    with k.phase():
        qT = k.sb("c_qT", [128, 4, T], BF16)
        kT = k.sb("c_kT", [128, 2, T], BF16)
        vaug = k.sb("c_vaug", [128, NT, 2, 66], BF16)
        sm = load_small(k, C, l, None)
        esink = k.sb("c_esink", [128, 8], F32)
        k.op("act", lambda e: e.activation(out=esink[:], in_=sm[:, 16:24], func=AF.Exp), R=["small_sb"], W=["c_esink"])
        psS = [k.ps(f"c_psS{i}", [128, 512], F32) for i in range(3)]
        with k.phase():
            xT = k.sb("c_xT", [128, KC, TP], BF16)
            wq = k.sb("c_wq", [128, KC, 512], BF16)
            wk = k.sb("c_wk", [128, KC, 256], BF16)
            wv = k.sb("c_wv", [128, KC, 128], BF16)
            k.dma("sp", xT[:].rearrange("p c t -> p (c t)"), C.xT_d[0], R=["xTd0", "xTd0h"], W=["c_xT"])
            k.dma("pool", wq[:], wsrc(C.w_in[l], CQ, 512), W=["c_wq"])
            k.dma("pool", wk[:], wsrc(C.w_ck[l], 0, 256), W=["c_wk"])
            k.dma("pool", wv[:], wsrc(C.w_in[l], CV, 128), W=["c_wv"])
            k.op("pool", lambda e: e.memset(vaug[:], 1.0), W=["c_vaug"])
            cnt = 0
            for j in range(6):
                for tb in range(C.NTB):
                    ps = psS[cnt % 3]
                    key = f"c_psS{cnt % 3}"
                    cnt += 1
                    wt, c0 = (wq, j * 128) if j < 4 else (wk, (j - 4) * 128)
                    for kc in range(KC):
                        k.op("pe", lambda e, ps=ps, kc=kc, wt=wt, c0=c0, tb=tb: e.matmul(
                            ps[:, 0:C.TB], lhsT=wt[:, kc, c0:c0 + 128], rhs=xT[:, kc, 2 + tb * C.TB:2 + (tb + 1) * C.TB],
                            start=(kc == 0), stop=(kc == KC - 1)), R=["c_xT", "c_wq", "c_wk"], W=[key])
                    if j < 4:
                        k.op("act", lambda e, ps=ps, j=j, tb=tb: e.activation(out=qT[:, j, tb * C.TB:(tb + 1) * C.TB], in_=ps[:, 0:C.TB], func=AF.Copy, scale=0.125),
                             R=[key], W=["c_qT"])
                    else:
                        k.op("act", lambda e, ps=ps, j=j, tb=tb: e.activation(out=kT[:, j - 4, tb * C.TB:(tb + 1) * C.TB], in_=ps[:, 0:C.TB], func=AF.Copy),
                             R=[key], W=["c_kT"])
            for tt in range(NT):
                ps = psS[cnt % 3]
                key = f"c_psS{cnt % 3}"
                cnt += 1
                for kc in range(KC):
                    k.op("pe", lambda e, ps=ps, kc=kc, tt=tt: e.matmul(
                        ps[:, 0:128], lhsT=xT[:, kc, 2 + tt * 128:2 + (tt + 1) * 128], rhs=wv[:, kc, :],
                        start=(kc == 0), stop=(kc == KC - 1)), R=["c_xT", "c_wv"], W=[key])
                k.op("dve", lambda e, ps=ps, tt=tt: e.tensor_copy(out=vaug[:, tt, :, 0:64], in_=ps[:, 0:128].rearrange("p (g d) -> p g d", g=2)),
                     R=[key], W=["c_vaug"])
        strip = k.sb("c_strip", [128, 8, 384], BF16)
        k.dma("pool", strip[:], C.stripC.rearrange("p (h w) -> p h w", h=8), W=["c_strip"], max_dma_last_dim=1536)
        PT = [k.sb(f"c_PT{i}", [128, NT, 384], BF16) for i in range(2)]
        po = [k.ps(f"c_po{i}", [128, 4, 128], F32) for i in range(2)]
        psT = k.ps("c_psT", [128, 8, 128], BF16)
        ytok = k.sb("c_ytok", [128, NT, 512], BF16)
        ycT = k.sb("c_ycT", [128, 4, T], BF16)
        rr = [k.sb(f"c_rr{i}", [128, 4], F32) for i in range(2)]
        scnt = [0]
        pcnt = [0]

        def emit_qk(h):
            g, half, j = h // 4, h % 2, h // 2
            pr = slice(64 * half, 64 * half + 64)
            buf = h % 2
            for kt in range(NT):
                qlo = max(0, kt - 1)
                qhi = min(NT, kt + 2)
                w = (qhi - qlo) * 128
                u0 = (qlo - (kt - 1)) * 128
                si = scnt[0] % 3
                scnt[0] += 1
                ps = psS[si]
                key = f"c_psS{si}"
                k.op("pe", lambda e, ps=ps, kt=kt, g=g, j=j, pr=pr, qlo=qlo, w=w: e.matmul(
                    ps[:, 0:w], lhsT=kT[pr, g, kt * 128:(kt + 1) * 128], rhs=qT[pr, j, qlo * 128:qlo * 128 + w], start=True, stop=False),
                    R=["c_qT", "c_kT"], W=[key])
                k.op("pe", lambda e, ps=ps, h=h, u0=u0, w=w: e.matmul(
                    ps[:, 0:w], lhsT=C.identb[:], rhs=strip[:, h, u0:u0 + w], start=False, stop=True), R=["identb", "c_strip"], W=[key])
                k.op("act", lambda e, ps=ps, buf=buf, kt=kt, u0=u0, w=w: e.activation(out=PT[buf][:, kt, u0:u0 + w], in_=ps[:, 0:w], func=AF.Exp),
                     R=[key], W=[f"c_PT{buf}_{kt}"])

        def emit_pv(h):
            g = h // 4
            buf = h % 2
            for q4 in range(0, NT, 4):
                pi = pcnt[0] % 2
                pcnt[0] += 1
                nq = min(4, NT - q4)
                for jq in range(nq):
                    qt = q4 + jq
                    kts = [kt for kt in (qt - 1, qt, qt + 1) if 0 <= kt < NT]
                    for i, kt in enumerate(kts):
                        b = qt - kt + 1
                        k.op("pe", lambda e, pi=pi, jq=jq, kt=kt, b=b, g=g, buf=buf, i=i, n=len(kts): e.matmul(
                            po[pi][:, jq, 0:65], lhsT=PT[buf][:, kt, b * 128:(b + 1) * 128], rhs=vaug[:, kt, g, 0:65],
                            start=(i == 0), stop=(i == n - 1)), R=[f"c_PT{buf}_{kt}", "c_vaug"], W=[f"c_po{pi}"])
                k.op("dve", lambda e, pi=pi, nq=nq, h=h: e.tensor_scalar(out=rr[pi][:, 0:nq], in0=po[pi][:, 0:nq, 64], scalar1=esink[:, h:h + 1], scalar2=None, op0=ALU.add),
                     R=[f"c_po{pi}", "c_esink"], W=[f"c_rr{pi}"])
                k.op("dve", lambda e, pi=pi, nq=nq: e.reciprocal(out=rr[pi][:, 0:nq], in_=rr[pi][:, 0:nq]), R=[f"c_rr{pi}"], W=[f"c_rr{pi}"])
                for jq in range(nq):
                    qt = q4 + jq
                    k.op("act", lambda e, pi=pi, jq=jq, qt=qt, h=h: e.activation(out=ytok[:, qt, h * 64:(h + 1) * 64], in_=po[pi][:, jq, 0:64], func=AF.Copy, scale=rr[pi][:, jq:jq + 1]),
                         R=[f"c_po{pi}", f"c_rr{pi}"], W=[f"c_ytok{qt}"])

        for h in range(9):
            if h < 8:
                emit_qk(h)
            if h >= 1:
                emit_pv(h - 1)
        for tt in range(NT):
            for j in range(4):
                k.op("pe", lambda e, tt=tt, j=j: e.transpose(out=psT[:, j, :], in_=ytok[:, tt, j * 128:(j + 1) * 128], identity=C.identb[:]),
                     R=[f"c_ytok{tt}", "identb"], W=["c_psT"])
            k.op("dve", lambda e, tt=tt: e.tensor_copy(out=ycT[:, :, tt * 128:(tt + 1) * 128], in_=psT[:, 0:4, :]), R=["c_psT"], W=["c_ycT"])
        k.dma("sp", C.yT_d[2], ycT[:].rearrange("p h t -> p (h t)"), R=["c_ycT"], W=["yTd2"])


def ln_tile(k, C, tag, ssum, lng, lnb, xo, xob, st, mv):
    for hh in range(2):
        k.op("dve", lambda e, hh=hh: e.bn_stats(out=st[:, hh, :], in_=ssum[:, hh * 512:(hh + 1) * 512]), R=[tag + "ssum"], W=[tag + f"st{hh}"])
    k.op("dve", lambda e: e.bn_aggr(out=mv[:, 0:2], in_=st[:, :, :].rearrange("p a b -> p (a b)")), R=[tag + "st0", tag + "st1"], W=[tag + "mv"])
    k.op("dve", lambda e: e.tensor_scalar(out=mv[:, 2:3], in0=mv[:, 1:2], scalar1=LN_EPS, scalar2=None, op0=ALU.add), R=[tag + "mv"], W=[tag + "mv2"])
    k.op("act", lambda e: e.activation(out=mv[:, 2:3], in_=mv[:, 2:3], func=AF.Sqrt), R=[tag + "mv2"], W=[tag + "mv3"])
    k.op("dve", lambda e: e.reciprocal(out=mv[:, 2:3], in_=mv[:, 2:3]), R=[tag + "mv3"], W=[tag + "mv4"])
    k.op("dve", lambda e: e.tensor_scalar(out=ssum[:], in0=ssum[:], scalar1=mv[:, 0:1], scalar2=mv[:, 2:3], op0=ALU.subtract, op1=ALU.mult),
         R=[tag + "ssum", tag + "mv", tag + "mv4"], W=[tag + "ssum"])
    k.op("pool", lambda e: e.tensor_tensor(out=ssum[:], in0=ssum[:], in1=lng[:], op=ALU.mult), R=[tag + "ssum", tag + "lng"], W=[tag + "ssum"])
    k.op("pool", lambda e: e.tensor_tensor(out=xo[:], in0=ssum[:], in1=lnb[:], op=ALU.add), R=[tag + "ssum", tag + "lnb"], W=[tag + "xo"])
    k.op("act", lambda e: e.activation(out=xob[:], in_=xo[:], func=AF.Copy), R=[tag + "xo"], W=[tag + "src"])


def phase_m(k, C, s, l):
    T, NT, TP, TB, NTB = C.T, C.NT, C.TP, C.TB, C.NTB
    with k.phase():
        mT = k.sb("m_mT", [128, KC, T], BF16)
        psA = [k.ps(f"m_psA{i}", [128, 512], F32) for i in range(2)]
        psG = [k.ps(f"m_psG{i}", [128, 512], F32) for i in range(2)]
        with k.phase():
            xT = k.sb("m_xT", [128, KC, TP], BF16)
            yT = k.sb("m_yT", [128, 12, T], BF16)
            k.dma("sp", xT[:].rearrange("p c t -> p (c t)"), C.xT_d[0], R=["xTd0", "xTd0h"], W=["m_xT"])
            for n in range(3):
                k.dma("sp", yT[:, n * 4:(n + 1) * 4, :].rearrange("p h t -> p (h t)"), C.yT_d[n], R=[f"yTd{n}"], W=["m_yT"])
            wbr = [k.sb(f"m_wbr{i}", [128, 12, 128], BF16) for i in range(2)]
            wg = [k.sb(f"m_wg{i}", [128, KC, 3, 128], BF16) for i in range(2)]
            sg = [k.sb(f"m_sg{i}", [128, TB], F32) for i in range(2)]
            macc = k.sb("m_macc", [128, TB], F32)
            mtmp = k.sb("m_mtmp", [128, TB], F32)
            cnt = 0
            for dc in range(KC):
                b = dc % 2
                for n in range(3):
                    k.dma("pool", wbr[b][:, n * 4:(n + 1) * 4, :], wsrc(C.w_br[l, n], dc * 128, 128), W=[f"m_wbr{b}"])
                    k.dma("pool", wg[b][:, :, n, :], wsrc(C.w_in[l], GZ + n * 1024 + dc * 128, 128), W=[f"m_wg{b}"])
                for tb in range(NTB):
                    for n in range(3):
                        pa, pg = psA[cnt % 2], psG[cnt % 2]
                        ka, kg, ks = f"m_psA{cnt % 2}", f"m_psG{cnt % 2}", f"m_sg{cnt % 2}"
                        sgt = sg[cnt % 2]
                        cnt += 1
                        for kc in range(KC):
                            k.op("pe", lambda e, pg=pg, kc=kc, n=n, b=b, tb=tb: e.matmul(
                                pg[:, 0:TB], lhsT=wg[b][:, kc, n, :], rhs=xT[:, kc, 2 + tb * TB:2 + (tb + 1) * TB],
                                start=(kc == 0), stop=(kc == KC - 1)), R=["m_xT", f"m_wg{b}"], W=[kg])
                        for kc in range(4):
                            k.op("pe", lambda e, pa=pa, kc=kc, n=n, b=b, tb=tb: e.matmul(
                                pa[:, 0:TB], lhsT=wbr[b][:, n * 4 + kc, :], rhs=yT[:, n * 4 + kc, tb * TB:(tb + 1) * TB],
                                start=(kc == 0), stop=(kc == 3)), R=["m_yT", f"m_wbr{b}"], W=[ka])
                        k.op("act", lambda e, pg=pg, sgt=sgt: e.activation(out=sgt[:], in_=pg[:, 0:TB], func=AF.Sigmoid), R=[kg], W=[ks])
                        if n == 0:
                            k.op("dve", lambda e, pa=pa, sgt=sgt: e.tensor_tensor(out=macc[:], in0=pa[:, 0:TB], in1=sgt[:], op=ALU.mult), R=[ka, ks], W=["m_macc"])
                        elif n == 1:
                            k.op("dve", lambda e, pa=pa, sgt=sgt: e.tensor_tensor(out=mtmp[:], in0=pa[:, 0:TB], in1=sgt[:], op=ALU.mult), R=[ka, ks], W=["m_mtmp"])
                            k.op("pool", lambda e: e.tensor_tensor(out=macc[:], in0=macc[:], in1=mtmp[:], op=ALU.add), R=["m_macc", "m_mtmp"], W=["m_macc"])
                        else:
                            k.op("dve", lambda e, pa=pa, sgt=sgt: e.tensor_tensor(out=mtmp[:], in0=pa[:, 0:TB], in1=sgt[:], op=ALU.mult), R=[ka, ks], W=["m_mtmp"])
                            k.op("pool", lambda e, dc=dc, tb=tb: e.tensor_tensor(out=mT[:, dc, tb * TB:(tb + 1) * TB], in0=macc[:], in1=mtmp[:], op=ALU.add),
                                 R=["m_macc", "m_mtmp"], W=["m_mT"])
        wo = k.sb("m_wo", [128, KC, D], BF16)
        for j in range(2):
            k.dma("pool", wo[:, :, j * 512:(j + 1) * 512], wsrc(C.w_out[l], j * 512, 512), W=["m_wo"])
        lng = k.sb("m_lng", [128, D], F32)
        lnb = k.sb("m_lnb", [128, D], F32)
        k.dma("sp", lng[:], C.lnp[l, 0:1, :].partition_broadcast(128), W=["mlng"])
        k.dma("sp", lnb[:], C.lnp[l, 1:2, :].partition_broadcast(128), W=["mlnb"])
        x1T = k.sb("m_x1T", [128, KC, TP], BF16)
        k.op("pool", lambda e: e.memset(x1T[:], 0.0), W=["mdst"])
        psT = k.ps("m_psT", [128, KC, 128], BF16)
        xr = [k.sb(f"m_xr{i}", [128, D], F32) for i in range(2)]
        ssum = k.sb("m_ssum", [128, D], F32)
        xo = [k.sb(f"m_xo{i}", [128, D], F32) for i in range(2)]
        xob = k.sb("m_xob", [128, D], BF16)
        st = k.sb("m_st", [128, 2, 6], F32)
        mv = k.sb("m_mv", [128, 4], F32)
        res_src = C.x[s] if l == 0 else C.xres_d[1]
        for tt in range(NT):
            b = tt % 2
            k.dma("sp", xr[b][:], res_src[tt * 128:(tt + 1) * 128, :], R=["xresd1"] if l > 0 else [], W=[f"m_xr{b}"])
            for hh in range(2):
                for kc in range(KC):
                    k.op("pe", lambda e, hh=hh, kc=kc, tt=tt: e.matmul(
                        psA[hh][:, :], lhsT=mT[:, kc, tt * 128:(tt + 1) * 128], rhs=wo[:, kc, hh * 512:(hh + 1) * 512],
                        start=(kc == 0), stop=(kc == KC - 1)), R=["m_mT", "m_wo"], W=[f"m_psA{hh}"])
                k.op("dve", lambda e, hh=hh, b=b: e.scalar_tensor_tensor(out=ssum[:, hh * 512:(hh + 1) * 512], in0=xr[b][:, hh * 512:(hh + 1) * 512], scalar=ALPHA,
                                                                      in1=psA[hh][:, :], op0=ALU.mult, op1=ALU.add), R=[f"m_xr{b}", f"m_psA{hh}"], W=["mssum"])
            ln_tile(k, C, "m", ssum, lng, lnb, xo[b], xob, st, mv)
            k.dma("sp", C.xres_d[0][tt * 128:(tt + 1) * 128, :], xo[b][:], R=["mxo"], W=["xresd0"])
            transpose_store(k, C, xob, x1T, 2 + tt * 128, psT, "m")
        k.dma("sp", C.xT_d[1], x1T[:].rearrange("p c t -> p (c t)"), R=["mdst"], W=["xTd1"])


def phase_f(k, C, s, l, last):
    T, NT, TP, TB, NTB = C.T, C.NT, C.TP, C.TB, C.NTB
    G = TB // 2 + 2
    with k.phase():
        x1T = k.sb("f_x1T", [128, KC, TP], BF16)
        k.dma("sp", x1T[:].rearrange("p c t -> p (c t)"), C.xT_d[1], R=["xTd1", "xTd1h"], W=["f_x1T"])
        wd = k.sb("f_wd", [128, NFC, D], BF16)
        for j in range(2):
            k.dma("pool", wd[:, :, j * 512:(j + 1) * 512], wsrc(C.w_dn[l], j * 512, 512), W=["f_wd"])
        fcv = k.sb("f_fcv", [128, NFC, 4], F32)
        k.dma("sp", fcv[:].rearrange("p a b -> p (a b)"), C.fconv[l], W=["f_fcv"])
        lng = k.sb("f_lng", [128, D], F32)
        lnb = k.sb("f_lnb", [128, D], F32)
        k.dma("sp", lng[:], C.lnp[l, 2:3, :].partition_broadcast(128), W=["flng"])
        k.dma("sp", lnb[:], C.lnp[l, 3:4, :].partition_broadcast(128), W=["flnb"])
        hT = k.sb("f_hT", [128, NFC, TB], BF16)
        wu = [k.sb(f"f_wu{i}", [128, KC, 2, 128], BF16) for i in range(2)]
        psg = [k.ps(f"f_psg{i}", [128, 2, 512], F32) for i in range(1)]
        psu = [k.ps(f"f_psu{i}", [128, 512], F32) for i in range(2)]
        psF = [k.ps(f"f_psF{i}", [128, 512], F32) for i in range(2)]
        psT = k.ps("f_psT", [128, KC, 128], BF16)
        gsb = [k.sb(f"f_gsb{i}", [128, 2, G], F32) for i in range(2)]
        cv = [k.sb(f"f_cv{i}", [128, TB], F32) for i in range(2)]
        xr = [k.sb(f"f_xr{i}", [128, D], F32) for i in range(2)]
        ssum = k.sb("f_ssum", [128, D], F32)
        xo = [k.sb(f"f_xo{i}", [128, D], F32) for i in range(2)]
        xob = k.sb("f_xob", [128, D], BF16)
        st = k.sb("f_st", [128, 2, 6], F32)
        mv = k.sb("f_mv", [128, 4], F32)
        xTo = k.sb("f_xTo", [128, KC, TB], BF16)
        xTd0v = C.xT_d[0].rearrange("p (c t) -> p c t", c=KC)
        for tb in range(NTB):
            b0 = tb * TB
            for fc in range(NFC):
                b = fc % 2
                k.dma("pool", wu[b][:, :, 0, :], wsrc(C.w_up[l], fc * 128, 128), W=[f"f_wu{b}"])
                k.dma("pool", wu[b][:, :, 1, :], wsrc(C.w_up[l], DFF + fc * 128, 128), W=[f"f_wu{b}"])
                pg = psg[0]
                for grp in range(2):
                    c0 = b0 + 1 + grp * (TB // 2)
                    for kc in range(KC):
                        k.op("pe", lambda e, pg=pg, grp=grp, kc=kc, b=b, c0=c0: e.matmul(
                            pg[:, grp, 0:G], lhsT=wu[b][:, kc, 0, :], rhs=x1T[:, kc, c0:c0 + G],
                            start=(kc == 0), stop=(kc == KC - 1)), R=["f_x1T", f"f_wu{b}"], W=["f_psg0"])
                pu = psu[fc % 2]
                for kc in range(KC):
                    k.op("pe", lambda e, pu=pu, kc=kc, b=b, b0=b0: e.matmul(
                        pu[:, 0:TB], lhsT=wu[b][:, kc, 1, :], rhs=x1T[:, kc, 2 + b0:2 + b0 + TB],
                        start=(kc == 0), stop=(kc == KC - 1)), R=["f_x1T", f"f_wu{b}"], W=[f"f_psu{fc % 2}"])
                gs = gsb[fc % 2]
                k.op("act", lambda e, gs=gs, pg=pg: e.activation(out=gs[:, :, :], in_=pg[:, :, 0:G], func=AF.Copy), R=["f_psg0"], W=[f"f_gsb{fc % 2}"])
                cvt = cv[fc % 2]
                H = TB // 2
                for half in range(2):
                    o = slice(half * H, (half + 1) * H)
                    k.op("dve", lambda e, gs=gs, cvt=cvt, half=half, o=o, fc=fc: e.tensor_scalar(
                        out=cvt[:, o], in0=gs[:, half, 0:H], scalar1=fcv[:, fc, 0:1], scalar2=fcv[:, fc, 3:4], op0=ALU.mult, op1=ALU.add),
                        R=[f"f_gsb{fc % 2}", "f_fcv"], W=[f"f_cv{fc % 2}_{half}"])
                    k.op("dve", lambda e, gs=gs, cvt=cvt, half=half, o=o, fc=fc: e.scalar_tensor_tensor(
                        out=cvt[:, o], in0=gs[:, half, 1:H + 1], scalar=fcv[:, fc, 1:2], in1=cvt[:, o], op0=ALU.mult, op1=ALU.add),
                        R=[f"f_gsb{fc % 2}", "f_fcv", f"f_cv{fc % 2}_{half}"], W=[f"f_cv{fc % 2}_{half}"])
                    k.op("dve", lambda e, gs=gs, cvt=cvt, half=half, o=o, fc=fc: e.scalar_tensor_tensor(
                        out=cvt[:, o], in0=gs[:, half, 2:H + 2], scalar=fcv[:, fc, 2:3], in1=cvt[:, o], op0=ALU.mult, op1=ALU.add),
                        R=[f"f_gsb{fc % 2}", "f_fcv", f"f_cv{fc % 2}_{half}"], W=[f"f_cv{fc % 2}_{half}"])
                k.op("act", lambda e, cvt=cvt: e.activation(out=cvt[:], in_=cvt[:], func=AF.Silu), R=[f"f_cv{fc % 2}_0", f"f_cv{fc % 2}_1"], W=[f"f_cvs{fc % 2}"])
                k.op("dve", lambda e, cvt=cvt, pu=pu, fc=fc: e.tensor_tensor(out=hT[:, fc, :], in0=pu[:, 0:TB], in1=cvt[:], op=ALU.mult),
                     R=[f"f_cvs{fc % 2}", f"f_psu{fc % 2}"], W=["f_hT", f"f_cv{fc % 2}_0", f"f_cv{fc % 2}_1"])
            for j in range(TB // 128):
                tt = tb * (TB // 128) + j
                b = tt % 2
                k.dma("sp", xr[b][:], C.xres_d[0][tt * 128:(tt + 1) * 128, :], R=["xresd0"], W=[f"f_xr{b}"])
                for hh in range(2):
                    for fc in range(NFC):
                        k.op("pe", lambda e, hh=hh, fc=fc, j=j: e.matmul(
                            psF[hh][:, :], lhsT=hT[:, fc, j * 128:(j + 1) * 128], rhs=wd[:, fc, hh * 512:(hh + 1) * 512],
                            start=(fc == 0), stop=(fc == NFC - 1)), R=["f_hT", "f_wd"], W=[f"f_psF{hh}"])
                    k.op("dve", lambda e, hh=hh, b=b: e.scalar_tensor_tensor(out=ssum[:, hh * 512:(hh + 1) * 512], in0=xr[b][:, hh * 512:(hh + 1) * 512], scalar=ALPHA,
                                                                          in1=psF[hh][:, :], op0=ALU.mult, op1=ALU.add), R=[f"f_xr{b}", f"f_psF{hh}"], W=["fssum"])
                ln_tile(k, C, "f", ssum, lng, lnb, xo[b], xob, st, mv)
                dst = C.out[s] if last else C.xres_d[1]
                k.dma("sp", dst[tt * 128:(tt + 1) * 128, :], xo[b][:], R=["fxo"], W=["xresd1"])
                if not last:
                    transpose_store(k, C, xob, xTo, j * 128, psT, "f")
            if not last:
                k.dma("sp", xTd0v[:, :, 2 + b0:2 + b0 + TB], xTo[:], R=["fdst"], W=["xTd0"])


def _bucket_tables():
    import jax
    import jax.numpy as jnp
    cpu = jax.devices("cpu")[0]
    with jax.default_device(cpu):
        rel = jnp.arange(-2100, 2101, dtype=jnp.int32)
        nb = 16
        ret = jnp.where(rel > 0, nb, 0)
        n = jnp.abs(rel)
        max_exact = nb // 2
        large = max_exact + (jnp.log(jnp.maximum(n, 1).astype(jnp.float32) / max_exact)
                             / math.log(128 / max_exact) * (nb - max_exact)).astype(jnp.int32)
        large = jnp.minimum(large, nb - 1)
        b = ret + jnp.where(n < max_exact, n, large)
        return np.asarray(b)


def host_prep(inp, L):
    f32 = np.float32
    bt = _bucket_tables()
    rb = np.asarray(inp["rel_bias"], f32)
    kl = np.arange(128)[:, None]
    j = np.arange(SA_W)[None, :]
    relA = kl - (j - 640)
    idxA = bt[relA + 2100]
    stripA = np.ascontiguousarray(np.transpose(rb[idxA][:, :, 0:4], (0, 2, 1))).reshape(128, 4 * SA_W)
    u = np.arange(384)[None, :]
    relC = kl + 128 - u
    idxC = bt[relC + 2100]
    sc = np.transpose(rb[idxC][:, :, 4:12], (0, 2, 1))
    valid = (np.abs(relC) <= 128)[:, None, :]
    stripC = np.ascontiguousarray(np.where(valid, sc, f32(NEG))).astype(f32).reshape(128, 8 * 384)
    w_in = np.asarray(inp["w_in"], f32)[:L]
    ck = w_in[:, :, CK:CK + 128]
    w_ck = np.ascontiguousarray(np.concatenate([ck[:, :, 0:64], ck[:, :, 0:64], ck[:, :, 64:128], ck[:, :, 64:128]], axis=2))
    lnp = np.ascontiguousarray(np.stack([inp["ln1_g"], inp["ln1_b"], inp["ln2_g"], inp["ln2_b"]], axis=1)).astype(f32)[:L]
    gconv = np.ascontiguousarray(np.asarray(inp["gdn_conv"], f32)[:L].reshape(L, 5, 12, 128).transpose(0, 3, 2, 1)).reshape(L, 128, 60)
    fc = np.asarray(inp["ffn_conv"], f32)[:L].reshape(L, 3, NFC, 128)
    fb = np.asarray(inp["ffn_conv_b"], f32)[:L].reshape(L, 1, NFC, 128)
    fconv = np.ascontiguousarray(np.concatenate([fc, fb], axis=1).transpose(0, 3, 2, 1)).reshape(L, 128, NFC * 4)
    small = np.ascontiguousarray(np.concatenate([
        np.asarray(inp["gdn_a_log"], f32)[:L].reshape(L, 8), np.asarray(inp["gdn_dt_bias"], f32)[:L].reshape(L, 8),
        np.asarray(inp["swa_sink"], f32)[:L].reshape(L, 8), np.asarray(inp["diff_lambda"], f32)[:L].reshape(L, 256)], axis=1))
    pcol = np.ascontiguousarray(np.stack([np.asarray(inp["diff_subln"], f32)[:L], np.asarray(inp["gdn_norm"], f32)[:L]], axis=2))
    shared = {
        "w_in": np.ascontiguousarray(w_in), "w_ck": w_ck,
        "w_branch": np.ascontiguousarray(np.asarray(inp["w_branch"], f32)[:L]),
        "w_out": np.ascontiguousarray(np.asarray(inp["w_out"], f32)[:L]),
        "ffn_up": np.ascontiguousarray(np.asarray(inp["ffn_up"], f32)[:L]),
        "ffn_down": np.ascontiguousarray(np.asarray(inp["ffn_down"], f32)[:L]),
        "lnp": lnp, "gconv": gconv, "fconv": fconv, "small": small, "pcol": pcol,
        "stripA": stripA.astype(f32), "stripC": stripC, "consts": make_consts(),
    }
    return shared


def kernel(**inputs):
    L, NS, T, NCORE = 2, 2, 2048, 8
    shared = host_prep(inputs, L)
    x = np.ascontiguousarray(np.asarray(inputs["x"], np.float32))
    nc = build(T, NS, L)
    in_maps = []
    for c in range(NCORE):
        m = dict(shared)
        m["x"] = np.ascontiguousarray(x[c * NS:(c + 1) * NS])
        in_maps.append(m)
    res = run_bass_kernel_spmd(nc, in_maps, core_ids=list(range(NCORE)))
    out = np.concatenate([np.asarray(r["out"]) for r in res.results], axis=0)
    return np.ascontiguousarray(out.astype(np.float32))
```
